# Optimizing a Trainium2 kernel written in Bass

```python
import math
import jax, jax.numpy as jnp
from jax import lax
import numpy as np

D_MODEL = 1024
BATCH = 4
SEQ = 4096
DEPTH = 1

MLA_HEADS = 8
MLA_Q_RANK = 256
MLA_KV_RANK = 128
MLA_NOPE = 64
MLA_ROPE = 32
MLA_V = 64
SWA_Q_HEADS = 16
SWA_KV_HEADS = 2
SWA_HEAD = 64
WINDOW = 128
BLOCK = 128
REL_BUCKETS = 32
REL_MAX_DIST = 128
D_FF = 4 * D_MODEL
ROPE_THETA = 10000.0
EPS = 1e-6

IN_SIZES = (MLA_Q_RANK, MLA_KV_RANK, MLA_ROPE,
            SWA_Q_HEADS * SWA_HEAD, SWA_KV_HEADS * SWA_HEAD, SWA_KV_HEADS * SWA_HEAD,
            D_MODEL, D_MODEL)
D_IN = sum(IN_SIZES)

kernel_name = "hybrid_mla_swa_gated_adaln_block"


def rmsnorm(x, g):
    xf = x.astype(jnp.float32)
    y = xf * lax.rsqrt(jnp.mean(xf * xf, axis=-1, keepdims=True) + EPS)
    return (y * g.astype(jnp.float32)).astype(x.dtype)


def modulate(h, shift, scale):
    return h * (1 + scale[:, None, :]) + shift[:, None, :]


def apply_rope(t, pos):
    half = t.shape[-1] // 2
    inv = ROPE_THETA ** (-jnp.arange(half, dtype=jnp.float32) / half)
    ang = pos.astype(jnp.float32)[..., None] * inv
    cos = jnp.cos(ang)[:, :, None, :]
    sin = jnp.sin(ang)[:, :, None, :]
    t1 = t[..., :half].astype(jnp.float32)
    t2 = t[..., half:].astype(jnp.float32)
    return jnp.concatenate([t1 * cos - t2 * sin, t1 * sin + t2 * cos], axis=-1).astype(t.dtype)


def rel_bucket(rel):
    n = jnp.maximum(rel, 0)
    max_exact = REL_BUCKETS // 2
    nf = jnp.maximum(n, 1).astype(jnp.float32)
    large = max_exact + (jnp.log(nf / max_exact) / math.log(REL_MAX_DIST / max_exact)
                         * (REL_BUCKETS - max_exact)).astype(jnp.int32)
    large = jnp.minimum(large, REL_BUCKETS - 1)
    return jnp.where(n < max_exact, n, large)


def mla_attention(q_lat, kv_lat, k_rope_raw, pos, g_q, g_kv, w_uq, w_ukv):
    B, S, _ = q_lat.shape
    q = jnp.einsum('bsr,rhd->bshd', rmsnorm(q_lat, g_q), w_uq)
    q_nope = q[..., :MLA_NOPE]
    q_pe = apply_rope(q[..., MLA_NOPE:], pos)
    kv = jnp.einsum('bsr,rhd->bshd', rmsnorm(kv_lat, g_kv), w_ukv)
    k_nope = kv[..., :MLA_NOPE]
    v = kv[..., MLA_NOPE:]
    k_pe = apply_rope(k_rope_raw[:, :, None, :], pos)[:, :, 0, :]
    scale = (MLA_NOPE + MLA_ROPE) ** -0.5
    nblk = S // BLOCK
    qn = q_nope.reshape(B, nblk, BLOCK, MLA_HEADS, MLA_NOPE).transpose(1, 0, 2, 3, 4)
    qp = q_pe.reshape(B, nblk, BLOCK, MLA_HEADS, MLA_ROPE).transpose(1, 0, 2, 3, 4)
    k_idx = jnp.arange(S)

    def one_block(args):
        i, qn_i, qp_i = args
        s = (jnp.einsum('bqhd,bkhd->bhqk', qn_i, k_nope)
             + jnp.einsum('bqhd,bkd->bhqk', qp_i, k_pe)).astype(jnp.float32) * scale
        q_idx = i * BLOCK + jnp.arange(BLOCK)
        mask = k_idx[None, :] <= q_idx[:, None]
        s = jnp.where(mask[None, None], s, -jnp.inf)
        p = jax.nn.softmax(s, axis=-1).astype(v.dtype)
        return jnp.einsum('bhqk,bkhd->bqhd', p, v)

    out = lax.map(one_block, (jnp.arange(nblk), qn, qp))
    return out.transpose(1, 0, 2, 3, 4).reshape(B, S, MLA_HEADS * MLA_V)


def swa_attention(q, k, v, pos, sinks, rel_bias):
    B, S = q.shape[0], q.shape[1]
    nblk = S // BLOCK
    G = SWA_Q_HEADS // SWA_KV_HEADS
    qb = q.reshape(B, nblk, BLOCK, SWA_KV_HEADS, G, SWA_HEAD)

    def band(t):
        tb = t.reshape((B, nblk, BLOCK) + t.shape[2:])
        pad = [(0, 0), (1, 0)] + [(0, 0)] * (tb.ndim - 2)
        prev = jnp.pad(tb, pad)[:, :-1]
        return jnp.concatenate([prev, tb], axis=2)

    kb, vb = band(k), band(v)
    s = jnp.einsum('bnqhgd,bnkhd->bnhgqk', qb, kb).astype(jnp.float32) * (SWA_HEAD ** -0.5)
    pq = pos.reshape(B, nblk, BLOCK)
    pk = band(pos)
    bucket = rel_bucket(pq[..., :, None] - pk[..., None, :])
    bias = rel_bias[:, bucket]
    bias = bias.reshape((SWA_KV_HEADS, G) + bias.shape[1:]).transpose(2, 3, 0, 1, 4, 5)
    s = s + bias.astype(jnp.float32)
    a = jnp.arange(BLOCK)[:, None]
    b = jnp.arange(2 * BLOCK)[None, :]
    dist = BLOCK + a - b
    band_ok = (dist >= 0) & (dist < WINDOW)
    blk = jnp.arange(nblk)[:, None, None]
    k_ok = (blk * BLOCK - BLOCK + b[None]) >= 0
    mask = band_ok[None] & k_ok
    s = jnp.where(mask[None, :, None, None], s, -jnp.inf)
    sink = jnp.broadcast_to(
        sinks.reshape(SWA_KV_HEADS, G)[None, None, :, :, None, None].astype(jnp.float32),
        s.shape[:-1] + (1,))
    p = jax.nn.softmax(jnp.concatenate([s, sink], axis=-1), axis=-1)[..., :-1]
    out = jnp.einsum('bnhgqk,bnkhd->bnqhgd', p.astype(vb.dtype), vb)
    return out.reshape(B, S, SWA_Q_HEADS * SWA_HEAD)


def setup_inputs(seed: int = 0) -> dict:
    key = jax.random.key(seed)
    ks = jax.random.split(key, 24)
    f32 = jnp.float32
    nrm = lambda k, shape, s: jax.random.normal(k, shape, f32) * s
    x = jax.random.normal(ks[0], (BATCH, SEQ, D_MODEL), f32)
    c = jax.random.normal(ks[1], (BATCH, D_MODEL), f32)
    start = jax.random.randint(ks[2], (BATCH, 1), 0, 1024, dtype=jnp.int32)
    positions = start + jnp.arange(SEQ, dtype=jnp.int32)[None, :]
    return {
        "x": x,
        "c": c,
        "positions": positions,
        "rel_bias": nrm(ks[3], (SWA_Q_HEADS, REL_BUCKETS), 0.5),
        "ada_w": nrm(ks[4], (DEPTH, D_MODEL, 6 * D_MODEL), 0.5 * D_MODEL ** -0.5),
        "ada_b": nrm(ks[5], (DEPTH, 6 * D_MODEL), 0.02),
        "ln_mix_g": 1.0 + nrm(ks[6], (DEPTH, D_MODEL), 0.01),
        "w_in": nrm(ks[7], (DEPTH, D_MODEL, D_IN), D_MODEL ** -0.5),
        "b_gate": nrm(ks[8], (DEPTH, 2 * D_MODEL), 0.02),
        "mla_q_norm_g": 1.0 + nrm(ks[9], (DEPTH, MLA_Q_RANK), 0.01),
        "mla_kv_norm_g": 1.0 + nrm(ks[10], (DEPTH, MLA_KV_RANK), 0.01),
        "w_uq": nrm(ks[11], (DEPTH, MLA_Q_RANK, MLA_HEADS, MLA_NOPE + MLA_ROPE), MLA_Q_RANK ** -0.5),
        "w_ukv": nrm(ks[12], (DEPTH, MLA_KV_RANK, MLA_HEADS, MLA_NOPE + MLA_V), MLA_KV_RANK ** -0.5),
        "swa_sinks": nrm(ks[13], (DEPTH, SWA_Q_HEADS), 0.5),
        "w_o_mla": nrm(ks[14], (DEPTH, MLA_HEADS * MLA_V, D_MODEL), (MLA_HEADS * MLA_V) ** -0.5),
        "w_o_swa": nrm(ks[15], (DEPTH, SWA_Q_HEADS * SWA_HEAD, D_MODEL), (SWA_Q_HEADS * SWA_HEAD) ** -0.5),
        "w_o": nrm(ks[16], (DEPTH, D_MODEL, D_MODEL), D_MODEL ** -0.5),
        "ln_mlp_g": 1.0 + nrm(ks[17], (DEPTH, D_MODEL), 0.01),
        "w_ff1": nrm(ks[18], (DEPTH, D_MODEL, D_FF), D_MODEL ** -0.5),
        "w_ff2": nrm(ks[19], (DEPTH, D_FF, D_MODEL), D_FF ** -0.5),
        "ln_final_g": 1.0 + nrm(ks[20], (D_MODEL,), 0.01),
    }


def reference(x, c, positions, rel_bias, ada_w, ada_b, ln_mix_g, w_in, b_gate,
              mla_q_norm_g, mla_kv_norm_g, w_uq, w_ukv, swa_sinks, w_o_mla, w_o_swa, w_o,
              ln_mlp_g, w_ff1, w_ff2, ln_final_g):
    B, S, D = x.shape
    HQ, HKV = SWA_Q_HEADS, SWA_KV_HEADS
    bounds = []
    acc = 0
    for sz in IN_SIZES[:-1]:
        acc += sz
        bounds.append(acc)
    for l in range(DEPTH):
        mod = jax.nn.silu(c) @ ada_w[l] + ada_b[l]
        sh1, sc1, ga1, sh2, sc2, ga2 = jnp.split(mod, 6, axis=-1)

        h = modulate(rmsnorm(x, ln_mix_g[l]), sh1, sc1)
        proj = h @ w_in[l]
        q_lat, kv_lat, k_rope, q_s, k_s, v_s, gl_a, gl_b = jnp.split(proj, bounds, axis=-1)

        y_mla = mla_attention(q_lat, kv_lat, k_rope, positions,
                              mla_q_norm_g[l], mla_kv_norm_g[l], w_uq[l], w_ukv[l])
        y_swa = swa_attention(q_s.reshape(B, S, HQ, SWA_HEAD),
                              k_s.reshape(B, S, HKV, SWA_HEAD),
                              v_s.reshape(B, S, HKV, SWA_HEAD),
                              positions, swa_sinks[l], rel_bias)
        g_a = jax.nn.sigmoid(gl_a + b_gate[l, :D])
        g_b = jax.nn.sigmoid(gl_b + b_gate[l, D:])
        merged = g_a * (y_mla @ w_o_mla[l]) + g_b * (y_swa @ w_o_swa[l])
        x = x + ga1[:, None, :] * (merged @ w_o[l])

        h = modulate(rmsnorm(x, ln_mlp_g[l]), sh2, sc2)
        u = jnp.square(jax.nn.relu(h @ w_ff1[l]))
        x = x + ga2[:, None, :] * (u @ w_ff2[l])
    return rmsnorm(x, ln_final_g)
```

```python
import math
from contextlib import ExitStack
import numpy as np
import concourse.bass as bass
import concourse.mybir as mybir
from concourse.bass_utils import run_bass_kernel_spmd

F32 = mybir.dt.float32
BF16 = mybir.dt.bfloat16
I32 = mybir.dt.int32
AF = mybir.ActivationFunctionType
ALU = mybir.AluOpType

D = 1024
S_LEN = 4096
T = 2048
GW = 640
EPS = 1e-6
NEG = -30000.0
EPOCH = 3000
TWO_PI = 2.0 * math.pi
CW1 = 6.28125
CW2 = TWO_PI - 6.28125
OWN_CHUNKS = ([0, 3, 4, 7], [1, 2, 5, 6])


class Sched:
    def __init__(self, nc, stack):
        self.nc = nc
        self.stack = stack
        self.eng = {"pe": nc.tensor, "act": nc.scalar, "dve": nc.vector,
                    "pool": nc.gpsimd, "sp": nc.sync}
        self.prog = {e: [] for e in self.eng}
        self.cnt = {e: 0 for e in self.eng}
        self.epoch = {e: 0 for e in self.eng}
        self.sems = {}
        self.seen = {e: {} for e in self.eng}
        self.bufs = {}
        self.pending = {e: ([], []) for e in self.eng}
        self.dma_cnt = {}
        self.nsem = 0

    def _deps(self, e, reads, writes):
        evs = []
        for b in reads:
            st = self.bufs.get(b)
            if st and st[0] is not None:
                evs.append(st[0])
        for b in writes:
            st = self.bufs.get(b)
            if st:
                if st[0] is not None:
                    evs.append(st[0])
                evs.extend(st[1])
        return self._filter(e, evs)

    def _filter(self, e, evs):
        best = {}
        for (key, val) in evs:
            if key[0] == "eng":
                src = key[1]
                if src == e and e == "pe":
                    continue
                cur = self.seen[e].get(("engmax", src), (-1, -1))
                if cur >= (key[2], val):
                    continue
                self.seen[e][("engmax", src)] = (key[2], val)
            else:
                if self.seen[e].get(key, -1) >= val:
                    continue
                self.seen[e][key] = val
            best[key] = max(best.get(key, -1), val)
        return list(best.items())

    def _commit(self, ev, reads, writes):
        for b in reads:
            self.bufs.setdefault(b, [None, []])[1].append(ev)
        for b in writes:
            self.bufs[b] = [ev, []]

    def op(self, e, fn, reads=(), writes=(), sig=True):
        waits = self._deps(e, reads, writes)
        if not sig:
            self.pending[e][0].extend(reads)
            self.pending[e][1].extend(writes)
            self.prog[e].append((waits, fn, None))
            return
        if self.cnt[e] >= EPOCH:
            self.epoch[e] += 1
            self.cnt[e] = 0
        self.cnt[e] += 1
        key = ("eng", e, self.epoch[e])
        ev = (key, self.cnt[e])
        pr, pw = self.pending[e]
        self._commit(ev, list(reads) + pr, list(writes) + pw)
        self.pending[e] = ([], [])
        self.prog[e].append((waits, fn, (key, 1)))

    def dma(self, e, out, in_, semkey, reads=(), writes=(), group=False, **kw):
        key = ("dma", semkey)
        if group:
            waits = [w for w in self._deps(e, reads, writes) if w[0] != key]
        else:
            waits = self._deps(e, reads, writes)
            prev = self.dma_cnt.get(key, 0)
            if prev > 0 and self.seen[e].get(key, -1) < prev and all(w[0] != key for w in waits):
                waits.append((key, prev))
                self.seen[e][key] = prev
        self.dma_cnt[key] = self.dma_cnt.get(key, 0) + 16
        ev = (key, self.dma_cnt[key])
        if group:
            self.seen[e].pop(key, None)
        self._commit(ev, reads, writes)

        def fn(eng, out=out, in_=in_, kw=kw):
            return eng.dma_start(out=out, in_=in_, **kw)
        self.prog[e].append((waits, fn, (key, 16)))

    def fence(self):
        comp = ["pe", "act", "dve", "pool"]
        for e in self.eng:
            evs = []
            for src in comp:
                if src == e:
                    continue
                if self.epoch[src] > 0 or self.cnt[src] > 0:
                    evs.append((("eng", src, self.epoch[src]), self.cnt[src]))
            for key, val in self.dma_cnt.items():
                name = key[1][0] if isinstance(key[1], tuple) else key[1]
                if name not in ("ring", "xst", "xst2", "wkvx", "ypk"):
                    evs.append((key, val))
            waits = self._filter(e, evs)
            if waits:
                self.prog[e].append((waits, None, None))

    def final_wait(self, e, bufs):
        waits = self._deps(e, bufs, ())
        self.prog[e].append((waits, None, None))

    def emit(self):
        keys = set()
        for e in self.prog:
            for (w, f, s) in self.prog[e]:
                for x in w:
                    keys.add(x[0])
                if s:
                    keys.add(s[0])
        for i, k in enumerate(sorted(keys, key=str)):
            self.sems[k] = self.stack.enter_context(self.nc.semaphore("s%d" % i))
        with self.nc.Block() as block:
            def run(e):
                def body(eng):
                    for (waits, fn, sig) in self.prog[e]:
                        for (k, v) in waits:
                            eng.wait_ge(self.sems[k], v)
                        if fn is None:
                            continue
                        ins = fn(eng)
                        if sig is not None:
                            ins.then_inc(self.sems[sig[0]], sig[1])
                return body
            block.tensor(run("pe"))
            block.scalar(run("act"))
            block.vector(run("dve"))
            block.gpsimd(run("pool"))
            block.sync(run("sp"))


class Arena:
    def __init__(self, base, limit, plan=None):
        self.base = base
        self.limit = limit
        self.plan = plan
        self.reqs = []

    def alloc(self, nbytes, p0, p1):
        nbytes = (nbytes + 63) // 64 * 64
        i = len(self.reqs)
        self.reqs.append((nbytes, p0, p1))
        if self.plan is None:
            return self.base
        assert self.plan["reqs"][i] == (nbytes, p0, p1), "allocation sequence changed between passes"
        return self.plan["offs"][i]

    def _place(self, order):
        placed = []
        offs = [None] * len(self.reqs)
        top = 0
        for i in order:
            n, p0, p1 = self.reqs[i]
            off = self.base
            while True:
                clash = None
                for (o, m, a, b) in placed:
                    if not (p1 < a or b < p0) and not (off + n <= o or o + m <= off):
                        clash = max(clash or 0, o + m)
                if clash is None:
                    break
                off = clash
            placed.append((off, n, p0, p1))
            offs[i] = off
            top = max(top, off + n)
        return top, offs

    def solve(self):
        idx = range(len(self.reqs))
        R = self.reqs
        keys = [lambda i: (-R[i][0], i),
                lambda i: (-(R[i][2] - R[i][1]), -R[i][0], i),
                lambda i: (-(R[i][2] - R[i][1] + 1) * R[i][0], i),
                lambda i: (R[i][1], -R[i][0], i),
                lambda i: (-R[i][2], -R[i][0], i)]
        best = None
        for k in keys:
            top, offs = self._place(sorted(idx, key=k))
            if best is None or top < best[0]:
                best = (top, offs)
        assert best[0] <= self.limit, ("SBUF arena overflow", best[0], self.limit)
        return {"reqs": list(self.reqs), "offs": best[1]}


def build_program(plan=None):
    nc = bass.Bass("TRN2", target_bir_lowering=False)

    def din(name, shape, dt=F32):
        return nc.dram_tensor(name, list(shape), dt, kind="ExternalInput").ap()

    xo_d = din("xo", [D, 4 * GW])
    xs_d = din("xs", [D, S_LEN])
    ct_d = din("cT", [128, 8])
    pos_d = din("pos", [1, S_LEN + T], I32)
    masks_d = din("masks", [16, 128, 512])
    hv_d = din("hv", [128, 4])
    adaw_d = din("ada_w", [D, 6 * D])
    adab_d = din("adabT", [128, 48])
    gains_d = din("gains", [128, 24])
    win_d = din("win2", [D, 3904])
    wuq_d = din("wuq", [256, 1536])
    wukv_d = din("wukv", [128, 1024])
    womla_d = din("w_o_mla", [512, D])
    woswa_d = din("w_o_swa_g", [D, D])
    wo_d = din("w_o", [D, D])
    wff1_d = din("w_ff1", [D, 4 * D])
    wff2_d = din("w_ff2", [4 * D, D])
    relT_d = din("relT", [32, 16])
    eoh_d = din("eoh", [32, 128])
    sinks_d = din("sinks", [1, 16])
    small_d = din("small", [128, 24])
    ident_d = din("ident", [128, 128])
    antiI_d = din("antiI", [128, 128])
    out_d = nc.dram_tensor("out", [D, T], F32, kind="ExternalOutput").ap()
    scr_d = nc.dram_tensor("scr", [16, 384], F32).ap()
    ropeq_d = nc.dram_tensor("ropeq_scr", [2, 32, T], F32).ap()
    bc_d = nc.dram_tensor("bc_scr", [2, 1, 512], F32).ap()

    with ExitStack() as st:
        S = Sched(nc, st)
        ARENA_BYTES = 207 * 1024
        arena = st.enter_context(nc.sbuf_tensor("arena", [128, ARENA_BYTES // 2], BF16))
        psum = [st.enter_context(nc.psum_tensor("psb%d" % i, [128, 512], F32)) for i in range(8)]
        A = Arena(0, ARENA_BYTES, plan)

        def V(off, dt, nelem, shape=None, parts=(0, 128)):
            esz = 2 if dt == BF16 else 4
            v = arena[parts[0]:parts[1], off // 2:(off + nelem * esz) // 2]
            if dt != BF16:
                v = v.bitcast(dt)
            if shape is not None:
                names = " ".join("d%d" % i for i in range(len(shape)))
                kw = {"d%d" % i: shape[i] for i in range(len(shape))}
                v = v.rearrange("p (%s) -> p %s" % (names, names), **kw)
            return v

        def PS(i):
            return psum[i]

        def mm(out, lhsT, rhs, start, stop, R, Wr, sig=None):
            if sig is None:
                sig = stop
            S.op("pe", lambda e: e.matmul(out, lhsT=lhsT, rhs=rhs, start=start, stop=stop),
                 reads=R, writes=Wr, sig=sig)

        def act(out, in_, func, R, Wr, **kw):
            S.op("act", lambda e: e.activation(out=out, in_=in_, func=func, **kw), reads=R, writes=Wr)

        def tt(eng, out, in0, in1, op, R, Wr):
            S.op(eng, lambda e: e.tensor_tensor(out=out, in0=in0, in1=in1, op=op), reads=R, writes=Wr)

        def ts(eng, out, in0, s1, s2, op0, op1, R, Wr):
            if s2 is None:
                S.op(eng, lambda e: e.tensor_scalar(out=out, in0=in0, scalar1=s1, scalar2=None, op0=op0),
                     reads=R, writes=Wr)
            else:
                S.op(eng, lambda e: e.tensor_scalar(out=out, in0=in0, scalar1=s1, scalar2=s2, op0=op0, op1=op1),
                     reads=R, writes=Wr)

        def stt(out, in0, scalar, in1, op0, op1, R, Wr):
            S.op("dve", lambda e: e.scalar_tensor_tensor(out=out, in0=in0, scalar=scalar, in1=in1, op0=op0, op1=op1),
                 reads=R, writes=Wr)

        def cp(eng, out, in_, R, Wr):
            S.op(eng, lambda e: e.tensor_copy(out=out, in_=in_), reads=R, writes=Wr)

        def memset(eng, ap, val, Wr):
            S.op(eng, lambda e: e.memset(ap, val), writes=Wr)

        P_ALL = (0, 9)
        o_cons = A.alloc(6144, *P_ALL)
        cb = [o_cons]

        def cons(dt, n, shape=None):
            esz = 2 if dt == BF16 else 4
            v = V(cb[0], dt, n, shape)
            cb[0] += (n * esz + 31) // 32 * 32
            assert cb[0] <= o_cons + 6144
            return v
        ones_bf = cons(BF16, 128)
        ident_bf = cons(BF16, 128)
        onesf = cons(F32, 128)
        antiI = cons(F32, 128)
        modT = cons(F32, 48)
        adab = cons(F32, 48)
        gains = cons(F32, 24)
        small = cons(F32, 24)
        cT = cons(F32, 8)
        scT = cons(BF16, 8)
        gm1 = cons(F32, 8)
        gm2 = cons(F32, 8)
        hv = cons(F32, 4)
        esink = cons(F32, 16)
        epst = cons(F32, 1)
        relT = cons(F32, 16)
        eoh = cons(F32, 128)
        sh2b = cons(BF16, 8)
        b1T = cons(F32, 32)
        bgT = small[:, 0:16]
        gqT = small[:, 16:18]
        gkvT = small[:, 18:19]
        ropec = small[:, 19:22]
        g1T = gains[:, 0:8]
        g2T = gains[:, 8:16]
        gfT = gains[:, 16:24]
        sh1 = modT[:, 0:8]
        ga1 = modT[:, 16:24]
        sh2 = modT[:, 24:32]
        ga2 = modT[:, 40:48]

        S.dma("sp", cT, ct_d, "c0", writes=["cT"])
        S.dma("sp", adab, adab_d, "c1", writes=["adab"])
        S.dma("sp", gains, gains_d, "c2", writes=["gains"])
        S.dma("sp", small, small_d, "c3", writes=["small"])
        S.dma("sp", hv, hv_d, "c4", writes=["hv"])
        S.dma("sp", relT[0:32, :], relT_d, "c5", writes=["relT"])
        S.dma("sp", eoh[0:32, :], eoh_d, "c6", writes=["eoh"])
        S.dma("sp", antiI, antiI_d, "c7", writes=["antiI"])
        S.dma("sp", esink, sinks_d.partition_broadcast(128), "c8", writes=["esink"])
        S.dma("pool", ident_bf, ident_d, "c9", writes=["ident"])
        memset("pool", ones_bf, 1.0, ["ones_bf"])
        memset("pool", onesf, 1.0, ["onesf"])
        memset("pool", epst, EPS, ["epst"])

        NSLOT = 3
        SLOT = 8192
        o_ring = A.alloc(NSLOT * SLOT, *P_ALL)
        pieces = []

        def slot_view(s, shape, parts=(0, 128), col0=0):
            n = 1
            for x in shape:
                n *= x
            return V(o_ring + s * SLOT + col0 * 2, BF16, n, shape, parts)

        class Ring:
            def __init__(self):
                self.next_load = 0
                self.next_rel = 0

            def _load(self):
                i = self.next_load
                if i >= len(pieces):
                    return
                s = i % NSLOT
                for (shape, parts, col0, src) in pieces[i]:
                    S.dma("pool", slot_view(s, shape, parts, col0), src, ("ring", s), writes=[("ring", s)],
                          group=(len(pieces[i]) > 1))
                self.next_load += 1

            def start(self):
                for _ in range(NSLOT):
                    self._load()

            def slot(self, i):
                assert i < self.next_load, (i, self.next_load)
                return i % NSLOT

            def release(self, i):
                assert i == self.next_rel
                self.next_rel += 1
                self._load()
        RING = Ring()

        def kc_src(w, c0, c1):
            return w.rearrange("(kc p) n -> p kc n", p=128)[:, :, c0:c1]

        def add_piece(subs):
            pieces.append(subs)
            return len(pieces) - 1

        PI_ADA = [None] * 12
        for i in range(4):
            PI_ADA[i] = add_piece([([8, 512], (0, 128), 0, kc_src(adaw_d, 512 * i, 512 * i + 512))])
        PI_WQ = add_piece([([8, 256], (0, 128), 0, kc_src(win_d, 0, 256))])
        for i in range(4, 12):
            PI_ADA[i] = add_piece([([8, 512], (0, 128), 0, kc_src(adaw_d, 512 * i, 512 * i + 512))])
        PI_QS = [None, None]
        PI_QS[0] = add_piece([([8, 512], (0, 128), 0, kc_src(win_d, 576, 576 + 512))])
        PI_KSVS = add_piece([([8, 256], (0, 128), 0, kc_src(win_d, 1600, 1856))])
        PI_QS[1] = add_piece([([8, 512], (0, 128), 0, kc_src(win_d, 576 + 512, 576 + 1024))])
        PI_MA = []
        for m in range(8):
            PI_MA.append(add_piece([
                ([8, 128], (0, 128), 0, kc_src(win_d, 1856 + 128 * m, 1856 + 128 * m + 128)),
                ([8, 128], (0, 128), 1024, kc_src(win_d, 2880 + 128 * m, 2880 + 128 * m + 128)),
                ([8, 128], (0, 128), 2048, kc_src(woswa_d, 128 * m, 128 * m + 128)),
                ([4, 128], (0, 128), 3072, womla_d.rearrange("(j p) n -> p j n", p=128)[:, :, 128 * m:128 * m + 128]),
            ]))
        PI_WO = [add_piece([([8, 512], (0, 128), 0, kc_src(wo_d, 512 * i, 512 * i + 512))]) for i in range(2)]
        PI_FF = []
        for hf in range(2):
            a = [add_piece([([8, 512], (0, 128), 0, kc_src(wff1_d, 2048 * hf + 512 * i, 2048 * hf + 512 * i + 512))])
                 for i in range(4)]
            b = [add_piece([([4, 1024], (0, 128), 0,
                             wff2_d.rearrange("(kc p) n -> p kc n", p=128)[:, 16 * hf + 4 * i:16 * hf + 4 * i + 4, :])])
                 for i in range(4)]
            PI_FF.append((a, b))
        RING.start()
        o_kvx = A.alloc(8 * 320 * 2, 0, 1)
        wkvx = V(o_kvx, BF16, 8 * 320, [8, 320])
        o_kvxf = A.alloc(8 * 320 * 4, 0, 0)
        wkvxf = V(o_kvxf, F32, 8 * 320, [8, 320])
        S.dma("sp", wkvxf, kc_src(win_d, 256, 576), "wkvx", writes=["wkvxf"])

        nrm_ctr = [0]

        def norm_scratch(p0, p1, ncols, nsq=2):
            return {"sq": A.alloc(nsq * 8 * ncols * 2, p0, p1), "tmp": A.alloc(4 * ncols * 4, p0, p1),
                    "rstd": A.alloc(2 * ncols * 4, p0, p1), "w": ncols, "nsq": nsq}

        def norm_sq(NS, xv, xkey, ncols, sq_eng="pool"):
            b = nrm_ctr[0] % 2
            nrm_ctr[0] += 1
            W_ = NS["w"]
            sqb = b % NS["nsq"]
            sq = V(NS["sq"] + sqb * 8 * W_ * 2, BF16, 8 * ncols, [8, ncols])
            if sq_eng == "act":
                act(sq, xv, AF.Square, [xkey], [("sq", sqb)])
            elif sq_eng == "pool_all":
                tt("pool", sq[:, 0:4, :], xv[:, 0:4, :], xv[:, 0:4, :], ALU.mult, [xkey], [("sq", sqb, 0)])
                tt("pool", sq[:, 4:8, :], xv[:, 4:8, :], xv[:, 4:8, :], ALU.mult, [xkey], [("sq", sqb, 1)])
            else:
                tt("pool", sq[:, 0:4, :], xv[:, 0:4, :], xv[:, 0:4, :], ALU.mult, [xkey], [("sq", sqb, 0)])
                act(sq[:, 4:8, :], xv[:, 4:8, :], AF.Square, [xkey], [("sq", sqb, 1)])
            return b

        def norm_ones(NS, b, ncols, subranges, psbanks):
            W_ = NS["w"]
            sqb = b % NS["nsq"]
            sq = V(NS["sq"] + sqb * 8 * W_ * 2, BF16, 8 * ncols, [8, ncols])
            for si, (c0, c1) in enumerate(subranges):
                pb = psbanks[si % len(psbanks)]
                for kc in range(8):
                    mm(PS(pb)[:, 0:c1 - c0], ones_bf, sq[:, kc, c0:c1], kc == 0, kc == 7,
                       [("sq", sqb), ("sq", sqb, 0), ("sq", sqb, 1), "ones_bf"], [("ps", pb)])

        def norm_stats_a(NS, xv, xkey, ncols, subranges, psbanks, sq_eng="pool"):
            b = norm_sq(NS, xv, xkey, ncols, sq_eng)
            norm_ones(NS, b, ncols, subranges, psbanks)
            return b

        def norm_stats_b(NS, b, ncols, subranges, psbanks):
            W_ = NS["w"]
            rstd = V(NS["rstd"] + b * W_ * 4, F32, ncols)
            for si, (c0, c1) in enumerate(subranges):
                pb = psbanks[si % len(psbanks)]
                act(rstd[:, c0:c1], PS(pb)[:, 0:c1 - c0], AF.Ln, [("ps", pb), "epst"], [("rstd", b)],
                    scale=1.0 / D, bias=epst[:, 0:1])
            act(rstd, rstd, AF.Exp, [("rstd", b)], [("rstd", b)], scale=-0.5)

        def norm_stats(NS, xv, xkey, ncols, subranges, psbanks, sq_eng="pool"):
            b = norm_stats_a(NS, xv, xkey, ncols, subranges, psbanks, sq_eng)
            norm_stats_b(NS, b, ncols, subranges, psbanks)
            return b

        def norm_apply(NS, b, xv, xkey, ncols, gm, sh, out_fn, okey, modkey, pool_kcs=()):
            W_ = NS["w"]
            rstd = V(NS["rstd"] + b * W_ * 4, F32, ncols)
            cnt = {"dve": 0, "pool": 0}
            for kc in range(8):
                eng = "pool" if kc in pool_kcs else "dve"
                tb = (cnt[eng] % 2) + (2 if eng == "pool" else 0)
                cnt[eng] += 1
                tmp = V(NS["tmp"] + tb * W_ * 4, F32, ncols)
                tt(eng, tmp, xv[:, kc, :], rstd, ALU.mult, [xkey, ("rstd", b)], [("ntmp", tb)])
                if sh is not None:
                    act(out_fn(kc), tmp, AF.Identity, [("ntmp", tb), modkey], [okey],
                        scale=gm[:, kc:kc + 1], bias=sh[:, kc:kc + 1])
                else:
                    act(out_fn(kc), tmp, AF.Identity, [("ntmp", tb), modkey], [okey], scale=gm[:, kc:kc + 1])

        o_bias = A.alloc(2 * 16 * 128 * 2, 0, 4)
        bias8 = V(o_bias, BF16, 2 * 16 * 128, [2, 16, 128])
        o_bt = A.alloc(2 * 16 * 128 * 4 + 384 * 4, 0, 0)
        btoe = V(o_bt, F32, 2 * 16 * 128, [2, 16, 128])
        wtab = V(o_bt + 2 * 16 * 128 * 4, F32, 384)
        mm(PS(6)[0:16, 0:128], relT[0:32, :], eoh[0:32, :], True, True, ["relT", "eoh"], [("ps", 6)])
        memset("pool", wtab[0:16, :], NEG * 8.0, ["wtab"])
        act(wtab[0:16, 127:255], PS(6)[0:16, 0:128], AF.Copy, [("ps", 6)], ["wtab"], scale=8.0)
        S.dma("sp", scr_d, wtab[0:16, :], "c11", reads=["wtab"], writes=["scr"])
        for sel, base in ((0, 255), (1, 127)):
            for hq in range(4):
                src = bass.AP(scr_d.tensor, base - 127 + 4 * hq * 384, [[1, 128], [384, 4], [1, 128]])
                S.dma("sp", btoe[:, sel, 4 * hq:4 * hq + 4, :], src, "c12", reads=["scr"], writes=["btoe"], group=True)

        act(scT, cT, AF.Silu, ["cT"], ["scT"])

        def ada_piece(i):
            s = RING.slot(PI_ADA[i])
            w = slot_view(s, [8, 512])
            for jj in range(4):
                j = 4 * i + jj
                for kc in range(8):
                    mm(PS(7)[:, j:j + 1], w[:, kc, 128 * jj:128 * jj + 128], scT[:, kc:kc + 1], kc == 0, kc == 7,
                       [("ring", s), "scT"], [("ps", 7)])
            tt("dve", modT[:, 4 * i:4 * i + 4], PS(7)[:, 4 * i:4 * i + 4], adab[:, 4 * i:4 * i + 4], ALU.add,
               [("ps", 7), "adab"], ["mod" if i < 4 else "mod2"])
            RING.release(PI_ADA[i])

        o_ropek = A.alloc(2 * S_LEN * 4, 0, 1)
        cosK = V(o_ropek, F32, S_LEN)
        sinK = V(o_ropek + S_LEN * 4, F32, S_LEN)
        RW = 1536
        o_rt = A.alloc(6 * RW * 4, 0, 0)
        RP = slice(64, 96)
        posi = V(o_rt, I32, RW)
        posf = V(o_rt + RW * 4, F32, RW)
        angt = V(o_rt + 2 * RW * 4, F32, RW)
        kf = V(o_rt + 3 * RW * 4, F32, RW)
        rtab = [V(o_rt + (4 + i) * RW * 4, F32, RW) for i in range(2)]
        for tg in range(4):
            pr = slice(32 * tg, 32 * tg + 32)
            S.dma("sp", posi[pr, 0:1024], pos_d[:, 1024 * tg:1024 * tg + 1024].partition_broadcast(32), "c10",
                  writes=["posi"], group=True)
            S.dma("sp", posi[pr, 1024:RW], pos_d[:, S_LEN + 512 * tg:S_LEN + 512 * tg + 512].partition_broadcast(32),
                  "c10", writes=["posi"], group=True)
        cp("dve", posf, posi, ["posi"], ["posf"])
        for col in range(2):
            ts("dve", angt, posf, ropec[:, 0:1], ropec[:, 1 + col:2 + col], ALU.mult, ALU.add,
               ["posf", "small"], ["angt"])
            ts("dve", kf, angt, 1.0 / TWO_PI, None, ALU.mult, None, ["angt"], ["kf"])
            cp("dve", posi, kf, ["kf"], ["kint", "posi"])
            cp("dve", kf, posi, ["kint"], ["kf"])
            stt(angt, kf, -CW1, angt, ALU.mult, ALU.add, ["kf", "angt"], ["angt"])
            stt(angt, kf, -CW2, angt, ALU.mult, ALU.add, ["kf", "angt"], ["angt"])
            ts("dve", kf, angt, math.pi, -TWO_PI, ALU.is_gt, ALU.mult, ["angt"], ["kf"])
            tt("dve", angt, angt, kf, ALU.add, ["angt", "kf"], ["angt"])
            ts("dve", kf, angt, -math.pi, TWO_PI, ALU.is_lt, ALU.mult, ["angt"], ["kf"])
            tt("dve", angt, angt, kf, ALU.add, ["angt", "kf"], ["angt"])
            ts("dve", angt, angt, math.pi, -math.pi, ALU.min, ALU.max, ["angt"], ["angt"])
            act(rtab[col], angt, AF.Sin, ["angt"], [("rtab", col)])
            tK = (cosK, sinK)[col]
            for tg in range(4):
                pr = slice(32 * tg, 32 * tg + 32)
                S.dma("sp", tK[RP, 1024 * tg:1024 * tg + 1024], rtab[col][pr, 0:1024], ("ropeK", col),
                      reads=[("rtab", col)], writes=["ropeK%d" % col], group=True)
                S.dma("sp", ropeq_d[col, :, 512 * tg:512 * tg + 512], rtab[col][pr, 1024:RW], ("ropeQs", col),
                      reads=[("rtab", col)], writes=["ropeQscr%d" % col], group=True)

        for q in range(8):
            flat = btoe.rearrange("p a b c -> p (a b c)")
            mm(PS(6)[:, :], antiI, flat[:, 512 * q:512 * q + 512], True, True, ["antiI", "btoe"], [("ps", 6)])
            cp("dve", bias8.rearrange("p a b c -> p (a b c)")[:, 512 * q:512 * q + 512], PS(6)[:, :],
               [("ps", 6)], ["bias8"])
        act(esink, esink, AF.Exp, ["esink"], ["esink"])
        o_xst = A.alloc(3 * 8 * 512 * 2, 0, 1)
        xbs = [V(o_xst + i * 8 * 512 * 2, BF16, 8 * 512, [8, 512]) for i in range(3)]

        def kv_load(G):
            S.dma("pool", xbs[G % 3], xs_d.rearrange("(kc p) n -> p kc n", p=128)[:, :, 512 * G:512 * G + 512],
                  ("xst", G % 3), writes=[("xst", G % 3)])
        for i in range(4):
            ada_piece(i)
        for G in range(3):
            kv_load(G)
        stt(gm1, modT[:, 8:16], 1.0, g1T, ALU.add, ALU.mult, ["mod", "gains"], ["mod"])
        bvk = cons(F32, 3)
        for (col, c0, M) in ((0, 0, 128), (1, 128, 96), (2, 224, 96)):
            for kc in range(8):
                mm(PS(6)[0:M, col:col + 1], wkvxf[:, kc, c0:c0 + M], sh1[:, kc:kc + 1], kc == 0, kc == 7,
                   ["wkvxf", "mod"], [("ps", 6)])
            cp("dve", bvk[0:M, col:col + 1], PS(6)[0:M, col:col + 1], [("ps", 6)], ["bvk"])
        for kc in range(8):
            ts("dve", wkvx[:, kc, :], wkvxf[:, kc, :], gm1[:, kc:kc + 1], None, ALU.mult, None,
               ["wkvxf", "mod"], ["wkvx"])

        S.fence()
        o_kvn = A.alloc(S_LEN * 2, 1, 3)
        o_K = A.alloc(2 * S_LEN * 2, 1, 3)
        o_kt = A.alloc(6 * 512 * 4 + 1024, 1, 1)
        o_sq1 = A.alloc(8 * 512 * 2, 1, 1)
        o_rs1 = A.alloc(2 * 512 * 4, 1, 1)
        kvn = V(o_kvn, BF16, S_LEN)
        Kb = [V(o_K + b * S_LEN * 2, BF16, S_LEN) for b in range(2)]
        sq1 = V(o_sq1, BF16, 8 * 512, [8, 512])
        rstd1 = [V(o_rs1 + i * 2048, F32, 512) for i in range(2)]

        def kv_sq(G):
            xb = xbs[G % 3]
            act(sq1[:, 0:4, :], xb[:, 0:4, :], AF.Square, [("xst", G % 3)], [("sq1", 0)])
            tt("dve", sq1[:, 4:8, :], xb[:, 4:8, :], xb[:, 4:8, :], ALU.mult, [("xst", G % 3)], [("sq1", 1)])

        def kv_ones(G):
            pb = (G % 2) * 5
            for kc in range(8):
                mm(PS(pb)[:, :], ones_bf, sq1[:, kc, :], kc == 0, kc == 7, [("sq1", 0), ("sq1", 1), "ones_bf"],
                   [("ps", pb)])
            r = rstd1[G % 2]
            act(r, PS(pb)[:, :], AF.Ln, [("ps", pb), "epst"], [("rstd1", G % 2)], scale=1.0 / D, bias=epst[:, 0:1])
            act(r, r, AF.Exp, [("rstd1", G % 2)], [("rstd1", G % 2)], scale=-0.5)

        def kv_proj(G):
            xb = xbs[G % 3]
            for (pb, c0, M) in ((1, 0, 128), (2, 128, 96), (3, 224, 96)):
                for kc in range(8):
                    mm(PS(pb)[0:M, :], wkvx[:, kc, c0:c0 + M], xb[:, kc, :], kc == 0, kc == 7,
                       ["wkvx", ("xst", G % 3)], [("ps", pb)])
            if G + 3 < 8:
                kv_load(G + 3)

        def kv_back(G):
            cs = slice(512 * G, 512 * G + 512)
            r = rstd1[G % 2]
            rkey = ("rstd1", G % 2)
            sqk = V(o_kt, BF16, 512)
            rk = V(o_kt + 1024, F32, 512)
            tkv = V(o_kt + 1024 + 2048, F32, 512)
            tb = G % 2
            t1 = V(o_kt + 1024 + 4096 + tb * 4096, F32, 512)
            t2 = V(o_kt + 1024 + 6144 + tb * 4096, F32, 512)
            tt("dve", tkv, PS(1)[:, :], r, ALU.mult, [("ps", 1), rkey], ["tkv"])
            tt("dve", t1[RP, :], PS(2)[RP, :], r[RP, :], ALU.mult, [("ps", 2), rkey], [("kt1", tb)])
            tt("dve", t2[RP, :], PS(3)[RP, :], r[RP, :], ALU.mult, [("ps", 3), rkey], [("kt2", tb)])
            if G > 0:
                kv_adds(G - 1)
            act(sqk, tkv, AF.Square, ["tkv", "bvk"], ["sqk"], bias=bvk[:, 0:1])
            mm(PS(4)[:, :], ones_bf, sqk, True, True, ["sqk", "ones_bf"], [("ps", 4)])
            act(rk, PS(4)[:, :], AF.Ln, [("ps", 4), "epst"], ["rk"], scale=1.0 / 128, bias=epst[:, 0:1])
            act(rk, rk, AF.Exp, ["rk"], ["rk"], scale=-0.5)
            stt(t1[RP, :], t1[RP, :], bvk[RP, 1:2], cosK[RP, cs], ALU.add, ALU.mult,
                [("kt1", tb), "bvk", "ropeK0"], [("kt1", tb)])
            stt(t2[RP, :], t2[RP, :], bvk[RP, 2:3], sinK[RP, cs], ALU.add, ALU.mult,
                [("kt2", tb), "bvk", "ropeK1"], [("kt2", tb)])
            stt(kvn[:, cs], tkv, bvk[:, 0:1], rk, ALU.add, ALU.mult, ["tkv", "rk", "bvk"], [("kvn", G)])

        def kv_adds(G):
            cs = slice(512 * G, 512 * G + 512)
            tb = G % 2
            t1 = V(o_kt + 1024 + 4096 + tb * 4096, F32, 512)
            t2 = V(o_kt + 1024 + 6144 + tb * 4096, F32, 512)
            tt("dve", Kb[0][RP, cs], t1[RP, :], t2[RP, :], ALU.add, [("kt1", tb), ("kt2", tb)], [("Kpe", 0, G)])
            tt("dve", Kb[1][RP, cs], t1[RP, :], t2[RP, :], ALU.add, [("kt1", tb), ("kt2", tb)], [("Kpe", 1, G)])

        kv_sq(0)
        kv_ones(0)
        for G in range(8):
            if G + 1 < 8:
                kv_sq(G + 1)
            kv_proj(G)
            if G + 1 < 8:
                kv_ones(G + 1)
            kv_back(G)
        kv_adds(7)
        o_xst2 = A.alloc(2 * 8 * GW * 4, 1, 2)
        xst2s = [V(o_xst2 + i * 8 * GW * 4, F32, 8 * GW, [8, GW]) for i in range(2)]

        def own_load(g):
            S.dma("sp", xst2s[g % 2], xo_d.rearrange("(kc p) n -> p kc n", p=128)[:, :, GW * g:GW * g + GW],
                  ("xst2", g % 2), writes=[("xst2", g % 2)])
        own_load(0)
        own_load(1)

        S.fence()
        o_ho = A.alloc(8 * 4 * GW * 2, 2, 5)
        o_qn = A.alloc(2 * T * 2, 2, 3)
        ho = V(o_ho, BF16, 8 * 4 * GW, [8, 4, GW])
        qn = V(o_qn, BF16, 2 * T, [2, T])
        o_wu = A.alloc(2 * 1536 * 2 + 1024 * 2, 2, 3)
        wuq = V(o_wu, BF16, 2 * 1536, [2, 1536])
        wukv = V(o_wu + 2 * 1536 * 2, BF16, 1024)
        o_wuf = A.alloc(1024 * 4, 2, 2)
        wukvf = V(o_wuf, F32, 1024)
        S.dma("sp", wukvf, wukv_d, "wukv", writes=["wukvf"])
        S.dma("pool", wuq, wuq_d.rearrange("(kc p) n -> p kc n", p=128), "wuq", writes=["wuq"])
        swq = RING.slot(PI_WQ)
        wq = slot_view(swq, [8, 256])
        o_qt = A.alloc(2 * 512 * 4 + 2 * 512 * 2, 2, 2)
        NS2 = norm_scratch(2, 2, GW, 1)
        sqq = V(o_qt, BF16, 2 * 512, [2, 512])
        rq = V(o_qt + 2048, F32, 512)

        own_nb = {}

        def own_sq(g):
            own_nb[g] = norm_sq(NS2, xst2s[g % 2], ("xst2", g % 2), GW)

        def own_ones(g):
            norm_ones(NS2, own_nb[g], GW, [(0, 128), (128, GW)], [0, 1])
            norm_stats_b(NS2, own_nb[g], GW, [(0, 128), (128, GW)], [0, 1])

        def own_apply(g):
            norm_apply(NS2, own_nb[g], xst2s[g % 2], ("xst2", g % 2), GW, gm1, sh1,
                       lambda kc, g=g: ho[:, kc, g, :], ("ho", g), "mod")
            if g + 2 < 4:
                own_load(g + 2)

        def own_proj(g):
            for mc in range(2):
                for kc in range(8):
                    mm(PS(2 + mc)[:, :], wq[:, kc, 128 * mc:128 * mc + 128], ho[:, kc, g, 128:GW], kc == 0, kc == 7,
                       [("ring", swq), ("ho", g)], [("ps", 2 + mc)])

        def own_back(g):
            for mc in range(2):
                act(sqq[:, mc, :], PS(2 + mc)[:, :], AF.Square, [("ps", 2 + mc)], ["sqq"])
            for mc in range(2):
                mm(PS(4)[:, :], ones_bf, sqq[:, mc, :], mc == 0, mc == 1, ["sqq", "ones_bf"], [("ps", 4)])
            act(rq, PS(4)[:, :], AF.Ln, [("ps", 4), "epst"], ["rq"], scale=1.0 / 256, bias=epst[:, 0:1])
            act(rq, rq, AF.Exp, ["rq"], ["rq"], scale=-0.5)
            for mc in range(2):
                stt(qn[:, mc, 512 * g:512 * g + 512], PS(2 + mc)[:, :], gqT[:, mc:mc + 1], rq, ALU.mult, ALU.mult,
                    [("ps", 2 + mc), "rq", "small"], [("qn", g)])

        own_sq(0)
        own_ones(0)
        own_apply(0)
        for g in range(4):
            if g + 1 < 4:
                own_sq(g + 1)
            own_proj(g)
            if g + 1 < 4:
                own_ones(g + 1)
            own_back(g)
            if g + 1 < 4:
                own_apply(g + 1)
        RING.release(PI_WQ)
        ts("dve", wukv, wukvf, gkvT[:, 0:1], None, ALU.mult, None, ["wukvf", "small"], ["wukv"])

        S.fence()
        SC_MLA = 96.0 ** -0.5
        o_ymla = A.alloc(8 * T * 2, 3, 5)
        cosQ = V(o_ymla, F32, T)
        sinQ = V(o_ymla + T * 4, F32, T)
        S.dma("sp", cosQ[RP, :], ropeq_d[0], "ropeQ0", reads=["ropeQscr0"], writes=["ropeQ0"])
        S.dma("sp", sinQ[RP, :], ropeq_d[1], "ropeQ1", reads=["ropeQscr1"], writes=["ropeQ1"])
        o_Vh = A.alloc(2 * 32 * 65 * 2, 3, 3)
        o_Q = A.alloc(2 * T * 2, 3, 3)
        o_mask = A.alloc(16 * 512 * 2, 3, 3)
        o_P = A.alloc(4 * 512 * 2, 3, 4)
        o_ep = A.alloc(4 * 512 * 4, 3, 3)
        Vh = [V(o_Vh + b * 32 * 65 * 2, BF16, 32 * 65, [32, 65]) for b in range(2)]
        Qb = [V(o_Q + b * T * 2, BF16, T) for b in range(2)]
        masks = V(o_mask, BF16, 16 * 512, [16, 512])
        Pt = [V(o_P + i * 1024, BF16, 512) for i in range(4)]
        rrows = [V(o_ep, F32, 512)] * 2
        bcs = [V(o_ep + 2048, F32, 512)] * 2
        ectr = [0]
        rt1 = V(o_ep + 2 * 2048, F32, 512)
        rt2 = V(o_ep + 3 * 2048, F32, 512)
        ymla = V(o_ymla, BF16, 8 * T, [8, T])
        for mq in range(4):
            S.dma("pool", masks[:, 4 * mq:4 * mq + 4, :], masks_d.rearrange("m p n -> p m n")[:, 4 * mq:4 * mq + 4, :],
                  "c13", writes=["masks"], group=True)
        for b in range(2):
            memset("pool", Vh[b][:, :, 64:65], 1.0, [("V", b)])
        pctr = [0]
        sctr = [0]
        actr = [0]
        bctr = [0]

        def build_steps(h):
            b = h % 2
            steps = []
            for G in range(8):
                def kstep(G=G):
                    pb = 5 + bctr[0] % 2
                    bctr[0] += 1
                    cs = slice(512 * G, 512 * G + 512)
                    mm(PS(pb)[0:64, :], wukv[:, 64 * h:64 * h + 64], kvn[:, cs], True, True,
                       ["wukv", ("kvn", G)], [("ps", pb)])
                    cp("dve", Kb[b][0:64, cs], PS(pb)[0:64, :], [("ps", pb)], [("K", b)])
                steps.append(kstep)
            for q4 in range(8):
                def vstep(q4=q4):
                    pb = 5 + bctr[0] % 2
                    bctr[0] += 1
                    for j in range(4):
                        blk = 4 * q4 + j
                        mm(PS(pb)[:, 64 * j:64 * j + 64], kvn[:, 128 * blk:128 * blk + 128],
                           wukv[:, 512 + 64 * h:512 + 64 * h + 64], True, True,
                           ["wukv", ("kvn", blk // 4)], [("ps", pb)], sig=(j == 3))
                    cp("dve", Vh[b][:, 4 * q4:4 * q4 + 4, 0:64],
                       PS(pb)[:, 0:256].rearrange("p (a b) -> p a b", a=4), [("ps", pb)], [("V", b)])
                steps.append(vstep)
            for g in range(4):
                def qstep(g=g):
                    cs = slice(512 * g, 512 * g + 512)
                    for (pb, c0) in ((5, 96 * h), (6, 768 + 96 * h)):
                        for kc in range(2):
                            mm(PS(pb)[0:96, :], wuq[:, kc, c0:c0 + 96], qn[:, kc, cs], kc == 0, kc == 1,
                               ["wuq", ("qn", g)], [("ps", pb)])
                    cp("dve", Qb[b][0:64, cs], PS(5)[0:64, :], [("ps", 5)], [("Q", b)])
                    tt("dve", rt1[RP, :], PS(5)[RP, :], cosQ[RP, cs], ALU.mult, [("ps", 5), "ropeQ0"], ["rt1"])
                    tt("dve", rt2[RP, :], PS(6)[RP, :], sinQ[RP, cs], ALU.mult, [("ps", 6), "ropeQ1"], ["rt2"])
                    tt("dve", Qb[b][RP, cs], rt1[RP, :], rt2[RP, :], ALU.add, ["rt1", "rt2"], [("Q", b)])
                steps.append(qstep)
            return steps

        items = []
        for h in range(8):
            for c in range(4):
                for kb in range(8 * c + 8):
                    items.append((h, c, kb))
        LOOK = 2
        st_sb = {}
        st_pi = {}
        st_ab = {}
        deferred = []
        for stp in build_steps(0):
            stp()
        pend_steps = []

        def mla_front(i):
            h, c, kb = items[i]
            b = h % 2
            if c == 0 and kb == 0:
                if h + 1 < 8:
                    pend_steps.extend(build_steps(h + 1))
            if c == 1 and kb == 0:
                ada_piece(4 + h)
            if kb == 0:
                st_ab[(h, c)] = actr[0] % 2
                actr[0] += 1
            sb = 2 + sctr[0] % 3
            sctr[0] += 1
            st_sb[i] = sb
            qs = slice(512 * c, 512 * c + 512)
            masked = kb >= 8 * c
            mm(PS(sb)[:, :], Kb[b][0:96, 128 * kb:128 * kb + 128], Qb[b][0:96, qs], True, not masked,
               [("K", b), ("Kpe", b, kb // 4), ("Q", b)], [("ps", sb)])
            if masked:
                mm(PS(sb)[:, :], ident_bf, masks[:, 8 * (c % 2) + kb - 8 * c, :], False, True,
                   ["ident", "masks"], [("ps", sb)])
            pi = pctr[0] % 4
            pctr[0] += 1
            st_pi[i] = pi
            act(Pt[pi], PS(sb)[:, :], AF.Exp, [("ps", sb)], [("P", pi)], scale=SC_MLA)

        def mla_back(i, now):
            h, c, kb = items[i]
            b = h % 2
            nkb = 8 * c + 8
            ab = st_ab[(h, c)]
            pi = st_pi[i]
            qs = slice(512 * c, 512 * c + 512)
            mm(PS(ab)[0:65, :], Vh[b][:, kb, :], Pt[pi], kb == 0, kb == nkb - 1,
               [("V", b), ("P", pi)], [("ps", ab)])
            if kb == nkb - 1:
                ri = 0
                rr = rrows[ri]
                S.op("dve", lambda e, ab=ab, rr=rr: e.reciprocal(out=rr[64:65, :], in_=PS(ab)[64:65, :]),
                     reads=[("ps", ab)], writes=[("rrow", ri)])
                S.dma("sp", bc_d[ri], rr[64:65, :], ("bcw", ri), reads=[("rrow", ri)], writes=[("bcd", ri)])
                S.dma("sp", bcs[ri][0:64, :], bc_d[ri].partition_broadcast(64), ("bcr", ri),
                      reads=[("bcd", ri)], writes=[("bcs", ri)])

                def stage_b(ab=ab, h=h, qs=qs, ri=ri):
                    tt("dve", ymla[0:64, h, qs], PS(ab)[0:64, :], bcs[ri][0:64, :], ALU.mult,
                       [("ps", ab), ("bcs", ri)], ["ymla"])
                deferred.append((now + 7, stage_b))

        n_items = len(items)
        for i in range(n_items + LOOK + 9):
            if i < n_items:
                mla_front(i)
            if LOOK <= i < n_items + LOOK:
                mla_back(i - LOOK, i)
            for (due, fn) in list(deferred):
                if due <= i:
                    deferred.remove((due, fn))
                    fn()
            if i % 3 == 2 and pend_steps:
                pend_steps.pop(0)()
        assert not deferred and not pend_steps
        for j in range(4):
            S.dma("sp", ymla[64:128, 2 * j, :], ymla[0:64, 2 * j + 1, :], ("ypk", j), reads=["ymla"],
                  writes=[("ymla_pk", j), "ropeQ0", "ropeQ1"])
        stt(gm2, modT[:, 32:40], 1.0, g2T, ALU.add, ALU.mult, ["mod2", "gains"], ["mod2"])
        cp("dve", sh2b, modT[:, 24:32], ["mod2"], ["sh2b"])

        S.fence()
        o_qs = A.alloc(4 * T * 2, 4, 4)
        o_ks = A.alloc(4 * GW * 2, 4, 4)
        o_vs = A.alloc(20 * 2 * 128 * 2, 4, 4)
        o_yswa = A.alloc(8 * T * 2, 4, 5)
        o_rs = A.alloc(4 * 512 * 4, 4, 4)
        qsT = V(o_qs, BF16, 4 * T, [4, T])
        ksT = V(o_ks, BF16, 4 * GW, [4, GW])
        vsP = V(o_vs, BF16, 20 * 2 * 128, [20, 2, 128])
        yswa = V(o_yswa, BF16, 8 * T, [8, T])
        memset("pool", vsP, 0.0, ["vsP"])

        def qs_project(i2):
            s = RING.slot(PI_QS[i2])
            w = slot_view(s, [8, 512])
            for mcl in range(4):
                for g in range(4):
                    pb = (mcl * 4 + g) % 4
                    for kc in range(8):
                        mm(PS(pb)[:, :], w[:, kc, 128 * mcl:128 * mcl + 128], ho[:, kc, g, 128:GW], kc == 0, kc == 7,
                           [("ring", s), ("ho", g)], [("ps", pb)])
                    if g % 2 == 0:
                        act(qsT[:, mcl, 512 * g:512 * g + 512], PS(pb)[:, :], AF.Copy, [("ps", pb)], ["qsT"])
                    else:
                        cp("dve", qsT[:, mcl, 512 * g:512 * g + 512], PS(pb)[:, :], [("ps", pb)], ["qsT"])
            RING.release(PI_QS[i2])

        qs_project(0)
        s = RING.slot(PI_KSVS)
        wkv2 = slot_view(s, [8, 256])
        for g in range(4):
            for (c0, c1, pb) in ((0, 512, 4), (512, GW, 5)):
                for kc in range(8):
                    mm(PS(pb)[:, 0:c1 - c0], wkv2[:, kc, 0:128], ho[:, kc, g, c0:c1], kc == 0, kc == 7,
                       [("ring", s), ("ho", g)], [("ps", pb)])
                cp("dve", ksT[:, g, c0:c1], PS(pb)[:, 0:c1 - c0], [("ps", pb)], ["ksT"])
            for j in range(5):
                blk = 5 * g + j
                pb = 6 + (blk % 2)
                for kc in range(8):
                    mm(PS(pb)[:, 0:128], ho[:, kc, g, 128 * j:128 * j + 128], wkv2[:, kc, 128:256], kc == 0, kc == 7,
                       [("ring", s), ("ho", g)], [("ps", pb)])
                for gg in range(2):
                    if j == 0:
                        ts("dve", vsP[:, blk, gg, 64 * gg:64 * gg + 64], PS(pb)[:, 64 * gg:64 * gg + 64],
                           hv[:, g:g + 1], None, ALU.mult, None, [("ps", pb), "hv"], ["vsP"])
                    else:
                        cp("dve", vsP[:, blk, gg, 64 * gg:64 * gg + 64], PS(pb)[:, 64 * gg:64 * gg + 64],
                           [("ps", pb)], ["vsP"])
        RING.release(PI_KSVS)
        o_hones = A.alloc(4 * 128 * 2, 4, 4)
        hones = V(o_hones, BF16, 4 * 128, [4, 128])
        for g in range(4):
            ts("dve", hones[:, g, :], ones_bf, hv[:, g:g + 1], None, ALU.mult, None, ["ones_bf", "hv"], ["hones"])
        sitems = [(n, half, gg, sel) for half in range(2) for n in range(16) for gg in range(2) for sel in range(2)]
        s_sb = {}
        s_pi = {}
        sdef = []

        def swa_front(i):
            n, half, gg, sel = sitems[i]
            g, j = n // 4, n % 4
            qcols = slice(512 * g + 128 * j, 512 * g + 128 * j + 128)
            rp = slice(64 * gg, 64 * gg + 64)
            if n == 0 and gg == 0 and sel == 0 and half == 1:
                qs_project(1)
            rhs_q = qsT[rp, 0:4, qcols]
            kcols = slice(128 * (j + sel), 128 * (j + sel) + 128)
            sb = sctr[0] % 4
            sctr[0] += 1
            mm(PS(sb)[:, :], ksT[rp, g, kcols], rhs_q, True, False, ["ksT", "qsT"], [("ps", sb)])
            h0 = 8 * gg + 4 * half
            mm(PS(sb)[:, :], ident_bf, bias8[:, sel, h0:h0 + 4, :], False, True, ["ident", "bias8"], [("ps", sb)])
            pi = pctr[0] % 4
            pctr[0] += 1
            s_pi[i] = pi
            act(Pt[pi], PS(sb)[:, :], AF.Exp, [("ps", sb)], [("P", pi)], scale=0.125)

        def swa_back(i, now):
            n, half, gg, sel = sitems[i]
            g, j = n // 4, n % 4
            qcols = slice(512 * g + 128 * j, 512 * g + 128 * j + 128)
            ab = 4 + n % 2
            sumb = 6 + gg
            blk = 5 * g + j + sel
            pi = s_pi[i]
            h0 = 8 * gg + 4 * half
            mm(PS(ab)[:, :], vsP[:, blk, gg, :], Pt[pi], (gg == 0 and sel == 0), (gg == 1 and sel == 1),
               ["vsP", ("P", pi)], [("ps", ab)])
            lhs1 = hones[:, g, :] if (j == 0 and sel == 0) else ones_bf
            mm(PS(sumb)[:, :], lhs1, Pt[pi], sel == 0, sel == 1, ["hones", "ones_bf", ("P", pi)], [("ps", sumb)])
            if sel == 1:
                ri = (2 * (n % 2) + gg) % 4
                rs = V(o_rs + ri * 2048, F32, 512)

                def stage_n(ri=ri, rs=rs, sumb=sumb, h0=h0):
                    tt("dve", rs.rearrange("p (a b) -> p a b", a=4), PS(sumb)[:, :].rearrange("p (a b) -> p a b", a=4),
                       esink[:, h0:h0 + 4].unsqueeze(2).broadcast_to([128, 4, 128]), ALU.add,
                       [("ps", sumb), "esink"], [("rs", ri)])
                    act(rs, rs, AF.Ln, [("rs", ri)], [("rs", ri)])
                    act(rs, rs, AF.Exp, [("rs", ri)], [("rs", ri)], scale=-1.0)
                sdef.append((now + 2, stage_n))
                if gg == 1:
                    def stage_y(ab=ab, half=half, qcols=qcols, n=n):
                        for g2 in range(2):
                            rp = slice(64 * g2, 64 * g2 + 64)
                            ri2 = (2 * (n % 2) + g2) % 4
                            rs2 = V(o_rs + ri2 * 2048, F32, 512)
                            tt("dve", yswa[rp, 4 * half:4 * half + 4, qcols],
                               PS(ab)[rp, :].rearrange("p (a b) -> p a b", a=4),
                               rs2[rp, :].rearrange("p (a b) -> p a b", a=4), ALU.mult,
                               [("ps", ab), ("rs", ri2)], ["yswa"])
                    sdef.append((now + 4, stage_y))

        ns = len(sitems)
        for i in range(ns + LOOK + 6):
            if i < ns:
                swa_front(i)
            if LOOK <= i < ns + LOOK:
                swa_back(i - LOOK, i)
            for (due, fn) in list(sdef):
                if due <= i:
                    sdef.remove((due, fn))
                    fn()
        assert not sdef

        S.fence()
        o_mg = A.alloc(8 * T * 2, 5, 6)
        o_gt = A.alloc(2 * 4 * 512 * 4, 5, 5)
        merged = V(o_mg, BF16, 8 * T, [8, T])
        it = 0
        for m in range(8):
            s = RING.slot(PI_MA[m])
            wga = slot_view(s, [8, 128], col0=0)
            wgb = slot_view(s, [8, 128], col0=1024)
            wsw = slot_view(s, [8, 128], col0=2048)
            wml = slot_view(s, [4, 128], col0=3072)
            for g in range(4):
                cs = slice(512 * g, 512 * g + 512)
                pbase = 4 * (it % 2)
                tb = it % 2
                it += 1
                gA = V(o_gt + tb * 8192, F32, 512)
                gB = V(o_gt + tb * 8192 + 2048, F32, 512)
                tA = V(o_gt + tb * 8192 + 4096, F32, 512)
                tB = V(o_gt + tb * 8192 + 6144, F32, 512)
                for kc in range(8):
                    mm(PS(pbase)[:, :], wga[:, kc, :], ho[:, kc, g, 128:GW], kc == 0, kc == 7,
                       [("ring", s), ("ho", g)], [("ps", pbase)])
                for kc in range(8):
                    mm(PS(pbase + 1)[:, :], wgb[:, kc, :], ho[:, kc, g, 128:GW], kc == 0, kc == 7,
                       [("ring", s), ("ho", g)], [("ps", pbase + 1)])
                for j in range(4):
                    mm(PS(pbase + 2)[:, :], wml[:, j, :], ymla[:, 2 * j, cs], j == 0, j == 3,
                       [("ring", s), "ymla", ("ymla_pk", j)], [("ps", pbase + 2)])
                for kc in range(8):
                    mm(PS(pbase + 3)[:, :], wsw[:, kc, :], yswa[:, kc, cs], kc == 0, kc == 7,
                       [("ring", s), "yswa"], [("ps", pbase + 3)])
                act(gA, PS(pbase)[:, :], AF.Sigmoid, [("ps", pbase), "small"], [("gA", tb)], bias=bgT[:, m:m + 1])
                act(gB, PS(pbase + 1)[:, :], AF.Sigmoid, [("ps", pbase + 1), "small"], [("gB", tb)],
                    bias=bgT[:, 8 + m:9 + m])
                tt("dve", tA, PS(pbase + 2)[:, :], gA, ALU.mult, [("ps", pbase + 2), ("gA", tb)], [("tA", tb)])
                tt("dve", tB, PS(pbase + 3)[:, :], gB, ALU.mult, [("ps", pbase + 3), ("gB", tb)], [("tB", tb)])
                tt("pool", merged[:, m, cs], tA, tB, ALU.add, [("tA", tb), ("tB", tb)], ["merged"])
            RING.release(PI_MA[m])

        S.fence()
        o_x1 = A.alloc(8 * T * 4, 6, 9)
        o_xre = A.alloc(2 * T * 4, 6, 6)
        x1 = V(o_x1, F32, 8 * T, [8, T])
        it = 0
        for i2 in range(2):
            s = RING.slot(PI_WO[i2])
            w = slot_view(s, [8, 512])
            for mcl in range(4):
                mc = 4 * i2 + mcl
                for g in range(4):
                    cs = slice(512 * g, 512 * g + 512)
                    pb = it % 4
                    xb = it % 2
                    it += 1
                    xb = mc % 2
                    xre4 = V(o_xre + xb * T * 4, F32, T, [4, 512])
                    if g == 0:
                        S.dma("sp", xre4, xo_d[128 * mc:128 * mc + 128, :].rearrange("p (g w) -> p g w", g=4)[:, :, 128:GW],
                              ("xre", xb), writes=[("xre", xb)])
                    xre = xre4[:, g, :]
                    for kc in range(8):
                        mm(PS(pb)[:, :], w[:, kc, 128 * mcl:128 * mcl + 128], merged[:, kc, cs], kc == 0, kc == 7,
                           [("ring", s), "merged"], [("ps", pb)])
                    stt(x1[:, mc, cs], PS(pb)[:, :], ga1[:, mc:mc + 1], xre, ALU.mult, ALU.add,
                        [("ps", pb), ("xre", xb), "mod2"], [("x1", g)])
            RING.release(PI_WO[i2])

        S.fence()
        o_h2 = A.alloc(8 * T * 2, 7, 8)
        h2 = V(o_h2, BF16, 8 * T, [8, T])
        NS7 = norm_scratch(7, 7, 512)
        nbs = {}
        nbs[0] = norm_stats(NS7, x1[:, :, 0:512], ("x1", 0), 512, [(0, 512)], [0])
        for g in range(4):
            cs = slice(512 * g, 512 * g + 512)
            if g + 1 < 4:
                cs1 = slice(512 * (g + 1), 512 * (g + 1) + 512)
                nbs[g + 1] = norm_stats(NS7, x1[:, :, cs1], ("x1", g + 1), 512, [(0, 512)], [(g + 1) % 2])
            rstd7 = V(NS7["rstd"] + nbs[g] * NS7["w"] * 4, F32, 512)
            for kc in range(8):
                stt(h2[:, kc, cs], x1[:, kc, cs], gm2[:, kc:kc + 1], rstd7, ALU.mult, ALU.mult,
                    [("x1", g), ("rstd", nbs[g]), "mod2"], [("h2", g)])

        S.fence()
        o_u = A.alloc(16 * T * 2, 8, 8)
        o_rl = A.alloc(2 * 512 * 4, 8, 8)
        u = V(o_u, BF16, 16 * T, [16, T])
        it = 0
        for hf in range(2):
            pa, pbb = PI_FF[hf]
            for i4 in range(4):
                s = RING.slot(pa[i4])
                w = slot_view(s, [8, 512])
                fg0 = 16 * hf + 4 * i4
                for fl in range(4):
                    for kc in range(8):
                        mm(PS(7)[:, fg0 + fl:fg0 + fl + 1], w[:, kc, 128 * fl:128 * fl + 128], sh2b[:, kc:kc + 1],
                           kc == 0, kc == 7, [("ring", s), "sh2b"], [("ps", 7)])
                cp("dve", b1T[:, fg0:fg0 + 4], PS(7)[:, fg0:fg0 + 4], [("ps", 7)], [("b1T", fg0)])
                for fl in range(4):
                    fc = 4 * i4 + fl
                    for g in range(4):
                        cs = slice(512 * g, 512 * g + 512)
                        pb = it % 4
                        rb = it % 2
                        it += 1
                        rl = V(o_rl + rb * 2048, F32, 512)
                        for kc in range(8):
                            mm(PS(pb)[:, :], w[:, kc, 128 * fl:128 * fl + 128], h2[:, kc, cs], kc == 0, kc == 7,
                               [("ring", s), ("h2", g)], [("ps", pb)])
                        act(rl, PS(pb)[:, :], AF.Relu, [("ps", pb), ("b1T", fg0)], [("rl", rb)],
                            bias=b1T[:, fg0 + fl:fg0 + fl + 1])
                        tt("dve", u[:, fc, cs], rl, rl, ALU.mult, [("rl", rb)], [("u", g)])
                RING.release(pa[i4])
            for i4 in range(4):
                s = RING.slot(pbb[i4])
                w2 = slot_view(s, [4, 1024])
                for m in range(8):
                    for g in range(4):
                        cs = slice(512 * g, 512 * g + 512)
                        pb = 4 + it % 3
                        it += 1
                        for kl in range(4):
                            mm(PS(pb)[:, :], w2[:, kl, 128 * m:128 * m + 128], u[:, 4 * i4 + kl, cs], kl == 0, kl == 3,
                               [("ring", s), ("u", g)], [("ps", pb)])
                        stt(x1[:, m, cs], PS(pb)[:, :], ga2[:, m:m + 1], x1[:, m, cs], ALU.mult, ALU.add,
                            [("ps", pb), ("x1", g), "mod2"], [("x1", g)])
                RING.release(pbb[i4])

        S.fence()
        o_ost = A.alloc(2 * 8 * 512 * 4, 9, 9)
        NS9 = norm_scratch(9, 9, 512)
        nbs = {}
        nbs[0] = norm_stats(NS9, x1[:, :, 0:512], ("x1", 0), 512, [(0, 512)], [0])
        for g in range(4):
            cs = slice(512 * g, 512 * g + 512)
            ob = g % 2
            ost = V(o_ost + ob * 8 * 512 * 4, F32, 8 * 512, [8, 512])
            if g + 1 < 4:
                cs1 = slice(512 * (g + 1), 512 * (g + 1) + 512)
                nbs[g + 1] = norm_stats(NS9, x1[:, :, cs1], ("x1", g + 1), 512, [(0, 512)], [(g + 1) % 2])
            rstd9 = V(NS9["rstd"] + nbs[g] * NS9["w"] * 4, F32, 512)
            for kc in range(8):
                stt(ost[:, kc, :], x1[:, kc, cs], gfT[:, kc:kc + 1], rstd9, ALU.mult, ALU.mult,
                    [("x1", g), ("rstd", nbs[g]), "gains"], [("ost", ob)])
            S.dma("sp", out_d.rearrange("(kc p) n -> p kc n", p=128)[:, :, cs], ost, ("outdma", ob),
                  reads=[("ost", ob)], writes=[("out", g)])
        S.final_wait("sp", [("out", g) for g in range(4)])
        if plan is None:
            return A.solve()
        S.emit()
    return nc


def _rel_bucket_onehot():
    n = np.arange(128)
    max_exact = 16
    nf = np.maximum(n, 1).astype(np.float32)
    large = max_exact + (np.log(nf / max_exact) / math.log(128 / max_exact) * (32 - max_exact)).astype(np.int32)
    large = np.minimum(large, 31)
    bucket = np.where(n < max_exact, n, large)
    E = np.zeros((32, 128), np.float32)
    E[bucket, n] = 1.0
    return E


def _mask_tiles(core_half):
    own = OWN_CHUNKS[core_half]
    out = np.zeros((16, 128, 512), np.float32)
    for par in range(2):
        c = par
        qidx = 512 * own[c] + np.arange(512)
        for m in range(8):
            kidx = 128 * (8 * c + m) + np.arange(128)
            ok = kidx[:, None] <= qidx[None, :]
            out[8 * par + m] = np.where(ok, 0.0, NEG)
    return out


def _prep_common(inp):
    f = np.float32
    w_in = np.asarray(inp["w_in"][0], f)
    z64 = np.zeros((D, 64), f)
    qs0 = 416
    qs_cols = []
    for i in range(8):
        qs_cols.append(w_in[:, qs0 + 64 * i:qs0 + 64 * i + 64])
        qs_cols.append(w_in[:, qs0 + 64 * (8 + i):qs0 + 64 * (8 + i) + 64])
    win2 = np.concatenate(
        [w_in[:, 0:256],
         w_in[:, 256:384], z64, w_in[:, 384:416], z64, w_in[:, 400:416], w_in[:, 384:400]]
        + qs_cols
        + [w_in[:, 1440:1568], w_in[:, 1568:1696], w_in[:, 1696:2720], w_in[:, 2720:3744]], axis=1)
    assert win2.shape == (D, 3904)
    w_uq = np.asarray(inp["w_uq"][0], f)
    wa = w_uq.reshape(256, 768)
    wb = np.zeros((256, 8, 96), f)
    wb[:, :, 64:80] = w_uq[:, :, 80:96]
    wb[:, :, 80:96] = w_uq[:, :, 64:80]
    wuq = np.concatenate([wa, wb.reshape(256, 768)], axis=1)
    w_ukv = np.asarray(inp["w_ukv"][0], f)
    wukv = np.concatenate([w_ukv[:, :, 0:64].reshape(128, 512), w_ukv[:, :, 64:128].reshape(128, 512)], axis=1)
    w_o_swa = np.asarray(inp["w_o_swa"][0], f)
    rows = []
    for i in range(8):
        rows.append(w_o_swa[64 * i:64 * i + 64])
        rows.append(w_o_swa[64 * (8 + i):64 * (8 + i) + 64])
    w_o_swa_g = np.concatenate(rows, axis=0)

    def colT(v, n):
        return np.ascontiguousarray(np.asarray(v, f).reshape(n, 128).T)
    gains = np.concatenate([colT(inp["ln_mix_g"][0], 8), colT(inp["ln_mlp_g"][0], 8), colT(inp["ln_final_g"], 8)], axis=1)
    small = np.zeros((128, 24), f)
    small[:, 0:16] = colT(inp["b_gate"][0], 16)
    small[:, 16:18] = colT(inp["mla_q_norm_g"][0], 2)
    small[:, 18:19] = colT(inp["mla_kv_norm_g"][0], 1)
    invf = (10000.0 ** (-np.arange(16, dtype=f) / 16)).astype(f)
    for tg in range(4):
        small[32 * tg:32 * tg + 16, 19] = invf
        small[32 * tg + 16:32 * tg + 32, 19] = invf
        small[32 * tg:32 * tg + 32, 20] = math.pi / 2
        small[32 * tg:32 * tg + 16, 21] = math.pi
    common = {
        "ada_w": np.ascontiguousarray(inp["ada_w"][0], f),
        "adabT": colT(inp["ada_b"][0], 48),
        "gains": np.ascontiguousarray(gains),
        "win2": np.ascontiguousarray(win2),
        "wuq": np.ascontiguousarray(wuq),
        "wukv": np.ascontiguousarray(wukv),
        "w_o_mla": np.ascontiguousarray(inp["w_o_mla"][0], f),
        "w_o_swa_g": np.ascontiguousarray(w_o_swa_g),
        "w_o": np.ascontiguousarray(inp["w_o"][0], f),
        "w_ff1": np.ascontiguousarray(inp["w_ff1"][0], f),
        "w_ff2": np.ascontiguousarray(inp["w_ff2"][0], f),
        "relT": np.ascontiguousarray(np.asarray(inp["rel_bias"], f).T),
        "eoh": _rel_bucket_onehot(),
        "sinks": np.ascontiguousarray(np.asarray(inp["swa_sinks"], f).reshape(1, 16)),
        "small": small,
        "ident": np.eye(128, dtype=f),
        "antiI": np.ascontiguousarray(np.eye(128, dtype=f)[::-1]),
    }
    return common


_NC_CACHE = {}


def kernel(**inputs):
    x = np.asarray(inputs["x"], np.float32)
    c = np.asarray(inputs["c"], np.float32)
    pos = np.asarray(inputs["positions"], np.int32)
    common = _prep_common(inputs)
    in_maps = []
    for core in range(8):
        b, half = core // 2, core % 2
        own = OWN_CHUNKS[half]
        xT = np.ascontiguousarray(x[b].T)
        xo = np.zeros((D, 4, GW), np.float32)
        hv = np.ones((128, 4), np.float32)
        opos = np.zeros((T,), np.int32)
        for lc, gc in enumerate(own):
            xo[:, lc, 128:] = xT[:, 512 * gc:512 * gc + 512]
            opos[512 * lc:512 * lc + 512] = pos[b, 512 * gc:512 * gc + 512]
            if gc == 0:
                hv[:, lc] = 0.0
            else:
                xo[:, lc, 0:128] = xT[:, 512 * gc - 128:512 * gc]
        m = dict(common)
        m["xo"] = np.ascontiguousarray(xo.reshape(D, 4 * GW))
        m["xs"] = xT
        m["cT"] = np.ascontiguousarray(c[b].reshape(8, 128).T)
        m["pos"] = np.ascontiguousarray(np.concatenate([pos[b], opos])[None, :].astype(np.int32))
        m["masks"] = _mask_tiles(half)
        m["hv"] = hv
        in_maps.append(m)
    if "nc" not in _NC_CACHE:
        _NC_CACHE["nc"] = build_program(build_program(None))
    res = run_bass_kernel_spmd(_NC_CACHE["nc"], in_maps, core_ids=list(range(8)))
    out = np.zeros((4, S_LEN, D), np.float32)
    for core in range(8):
        b, half = core // 2, core % 2
        oT = np.asarray(res.results[core]["out"], np.float32)
        for lc, gc in enumerate(OWN_CHUNKS[half]):
            out[b, 512 * gc:512 * gc + 512, :] = oT[:, 512 * lc:512 * lc + 512].T
    return out
```

```python
import math
from contextlib import ExitStack
import numpy as np
import concourse.bass as bass
import concourse.mybir as mybir
from concourse.bass_utils import run_bass_kernel_spmd

F32 = mybir.dt.float32
BF16 = mybir.dt.bfloat16
I32 = mybir.dt.int32
AF = mybir.ActivationFunctionType
ALU = mybir.AluOpType

D = 1024
S_LEN = 4096
T = 2048
GW = 640
EPS = 1e-6
NEG = -30000.0
EPOCH = 3000
TWO_PI = 2.0 * math.pi
CW1 = 6.28125
CW2 = TWO_PI - 6.28125
OWN_CHUNKS = ([0, 3, 4, 7], [1, 2, 5, 6])


class Sched:
    def __init__(self, nc, stack):
        self.nc = nc
        self.stack = stack
        self.eng = {"pe": nc.tensor, "act": nc.scalar, "dve": nc.vector,
                    "pool": nc.gpsimd, "sp": nc.sync}
        self.prog = {e: [] for e in self.eng}
        self.cnt = {e: 0 for e in self.eng}
        self.epoch = {e: 0 for e in self.eng}
        self.sems = {}
        self.seen = {e: {} for e in self.eng}
        self.bufs = {}
        self.pending = {e: ([], []) for e in self.eng}
        self.dma_cnt = {}
        self.nsem = 0

    def _deps(self, e, reads, writes):
        evs = []
        for b in reads:
            st = self.bufs.get(b)
            if st and st[0] is not None:
                evs.append(st[0])
        for b in writes:
            st = self.bufs.get(b)
            if st:
                if st[0] is not None:
                    evs.append(st[0])
                evs.extend(st[1])
        return self._filter(e, evs)

    def _filter(self, e, evs):
        best = {}
        for (key, val) in evs:
            if key[0] == "eng":
                src = key[1]
                if src == e and e == "pe":
                    continue
                cur = self.seen[e].get(("engmax", src), (-1, -1))
                if cur >= (key[2], val):
                    continue
                self.seen[e][("engmax", src)] = (key[2], val)
            else:
                if self.seen[e].get(key, -1) >= val:
                    continue
                self.seen[e][key] = val
            best[key] = max(best.get(key, -1), val)
        return list(best.items())

    def _commit(self, ev, reads, writes):
        for b in reads:
            self.bufs.setdefault(b, [None, []])[1].append(ev)
        for b in writes:
            self.bufs[b] = [ev, []]

    def op(self, e, fn, reads=(), writes=(), sig=True):
        waits = self._deps(e, reads, writes)
        if not sig:
            self.pending[e][0].extend(reads)
            self.pending[e][1].extend(writes)
            self.prog[e].append((waits, fn, None))
            return
        if self.cnt[e] >= EPOCH:
            self.epoch[e] += 1
            self.cnt[e] = 0
        self.cnt[e] += 1
        key = ("eng", e, self.epoch[e])
        ev = (key, self.cnt[e])
        pr, pw = self.pending[e]
        self._commit(ev, list(reads) + pr, list(writes) + pw)
        self.pending[e] = ([], [])
        self.prog[e].append((waits, fn, (key, 1)))

    def dma(self, e, out, in_, semkey, reads=(), writes=(), group=False, **kw):
        key = ("dma", semkey)
        if group:
            waits = [w for w in self._deps(e, reads, writes) if w[0] != key]
        else:
            waits = self._deps(e, reads, writes)
            prev = self.dma_cnt.get(key, 0)
            if prev > 0 and self.seen[e].get(key, -1) < prev and all(w[0] != key for w in waits):
                waits.append((key, prev))
                self.seen[e][key] = prev
        self.dma_cnt[key] = self.dma_cnt.get(key, 0) + 16
        ev = (key, self.dma_cnt[key])
        if group:
            self.seen[e].pop(key, None)
        self._commit(ev, reads, writes)

        def fn(eng, out=out, in_=in_, kw=kw):
            return eng.dma_start(out=out, in_=in_, **kw)
        self.prog[e].append((waits, fn, (key, 16)))

    def fence(self):
        comp = ["pe", "act", "dve", "pool"]
        for e in self.eng:
            evs = []
            for src in comp:
                if src == e:
                    continue
                if self.epoch[src] > 0 or self.cnt[src] > 0:
                    evs.append((("eng", src, self.epoch[src]), self.cnt[src]))
            for key, val in self.dma_cnt.items():
                name = key[1][0] if isinstance(key[1], tuple) else key[1]
                if name not in ("ring", "xst", "xst2", "wkvx", "ypk"):
                    evs.append((key, val))
            waits = self._filter(e, evs)
            if waits:
                self.prog[e].append((waits, None, None))

    def final_wait(self, e, bufs):
        waits = self._deps(e, bufs, ())
        self.prog[e].append((waits, None, None))

    def emit(self):
        keys = set()
        for e in self.prog:
            for (w, f, s) in self.prog[e]:
                for x in w:
                    keys.add(x[0])
                if s:
                    keys.add(s[0])
        for i, k in enumerate(sorted(keys, key=str)):
            self.sems[k] = self.stack.enter_context(self.nc.semaphore("s%d" % i))
        with self.nc.Block() as block:
            def run(e):
                def body(eng):
                    for (waits, fn, sig) in self.prog[e]:
                        for (k, v) in waits:
                            eng.wait_ge(self.sems[k], v)
                        if fn is None:
                            continue
                        ins = fn(eng)
                        if sig is not None:
                            ins.then_inc(self.sems[sig[0]], sig[1])
                return body
            block.tensor(run("pe"))
            block.scalar(run("act"))
            block.vector(run("dve"))
            block.gpsimd(run("pool"))
            block.sync(run("sp"))


class Arena:
    def __init__(self, base, limit, plan=None):
        self.base = base
        self.limit = limit
        self.plan = plan
        self.reqs = []

    def alloc(self, nbytes, p0, p1):
        nbytes = (nbytes + 63) // 64 * 64
        i = len(self.reqs)
        self.reqs.append((nbytes, p0, p1))
        if self.plan is None:
            return self.base
        assert self.plan["reqs"][i] == (nbytes, p0, p1), "allocation sequence changed between passes"
        return self.plan["offs"][i]

    def _place(self, order):
        placed = []
        offs = [None] * len(self.reqs)
        top = 0
        for i in order:
            n, p0, p1 = self.reqs[i]
            off = self.base
            while True:
                clash = None
                for (o, m, a, b) in placed:
                    if not (p1 < a or b < p0) and not (off + n <= o or o + m <= off):
                        clash = max(clash or 0, o + m)
                if clash is None:
                    break
                off = clash
            placed.append((off, n, p0, p1))
            offs[i] = off
            top = max(top, off + n)
        return top, offs

    def solve(self):
        idx = range(len(self.reqs))
        R = self.reqs
        keys = [lambda i: (-R[i][0], i),
                lambda i: (-(R[i][2] - R[i][1]), -R[i][0], i),
                lambda i: (-(R[i][2] - R[i][1] + 1) * R[i][0], i),
                lambda i: (R[i][1], -R[i][0], i),
                lambda i: (-R[i][2], -R[i][0], i)]
        best = None
        for k in keys:
            top, offs = self._place(sorted(idx, key=k))
            if best is None or top < best[0]:
                best = (top, offs)
        assert best[0] <= self.limit, ("SBUF arena overflow", best[0], self.limit)
        return {"reqs": list(self.reqs), "offs": best[1]}


def build_program(plan=None):
    nc = bass.Bass("TRN2", target_bir_lowering=False)

    def din(name, shape, dt=F32):
        return nc.dram_tensor(name, list(shape), dt, kind="ExternalInput").ap()

    xo_d = din("xo", [D, 4 * GW])
    xs_d = din("xs", [D, S_LEN])
    ct_d = din("cT", [128, 8])
    pos_d = din("pos", [1, S_LEN + T], I32)
    masks_d = din("masks", [16, 128, 512])
    hv_d = din("hv", [128, 4])
    adaw_d = din("ada_w", [D, 6 * D])
    adab_d = din("adabT", [128, 48])
    gains_d = din("gains", [128, 24])
    win_d = din("win2", [D, 3904])
    wuq_d = din("wuq", [256, 1536])
    wukv_d = din("wukv", [128, 1024])
    womla_d = din("w_o_mla", [512, D])
    woswa_d = din("w_o_swa_g", [D, D])
    wo_d = din("w_o", [D, D])
    wff1_d = din("w_ff1", [D, 4 * D])
    wff2_d = din("w_ff2", [4 * D, D])
    relT_d = din("relT", [32, 16])
    eoh_d = din("eoh", [32, 128])
    sinks_d = din("sinks", [1, 16])
    small_d = din("small", [128, 24])
    ident_d = din("ident", [128, 128])
    antiI_d = din("antiI", [128, 128])
    out_d = nc.dram_tensor("out", [D, T], F32, kind="ExternalOutput").ap()
    scr_d = nc.dram_tensor("scr", [16, 384], F32).ap()
    ropeq_d = nc.dram_tensor("ropeq_scr", [2, 32, T], F32).ap()
    bc_d = nc.dram_tensor("bc_scr", [2, 1, 512], F32).ap()

    with ExitStack() as st:
        S = Sched(nc, st)
        ARENA_BYTES = 207 * 1024
        arena = st.enter_context(nc.sbuf_tensor("arena", [128, ARENA_BYTES // 2], BF16))
        psum = [st.enter_context(nc.psum_tensor("psb%d" % i, [128, 512], F32)) for i in range(8)]
        A = Arena(0, ARENA_BYTES, plan)

        def V(off, dt, nelem, shape=None, parts=(0, 128)):
            esz = 2 if dt == BF16 else 4
            v = arena[parts[0]:parts[1], off // 2:(off + nelem * esz) // 2]
            if dt != BF16:
                v = v.bitcast(dt)
            if shape is not None:
                names = " ".join("d%d" % i for i in range(len(shape)))
                kw = {"d%d" % i: shape[i] for i in range(len(shape))}
                v = v.rearrange("p (%s) -> p %s" % (names, names), **kw)
            return v

        def PS(i):
            return psum[i]

        def mm(out, lhsT, rhs, start, stop, R, Wr, sig=None):
            if sig is None:
                sig = stop
            S.op("pe", lambda e: e.matmul(out, lhsT=lhsT, rhs=rhs, start=start, stop=stop),
                 reads=R, writes=Wr, sig=sig)

        def act(out, in_, func, R, Wr, **kw):
            S.op("act", lambda e: e.activation(out=out, in_=in_, func=func, **kw), reads=R, writes=Wr)

        def tt(eng, out, in0, in1, op, R, Wr):
            S.op(eng, lambda e: e.tensor_tensor(out=out, in0=in0, in1=in1, op=op), reads=R, writes=Wr)

        def ts(eng, out, in0, s1, s2, op0, op1, R, Wr):
            if s2 is None:
                S.op(eng, lambda e: e.tensor_scalar(out=out, in0=in0, scalar1=s1, scalar2=None, op0=op0),
                     reads=R, writes=Wr)
            else:
                S.op(eng, lambda e: e.tensor_scalar(out=out, in0=in0, scalar1=s1, scalar2=s2, op0=op0, op1=op1),
                     reads=R, writes=Wr)

        def stt(out, in0, scalar, in1, op0, op1, R, Wr):
            S.op("dve", lambda e: e.scalar_tensor_tensor(out=out, in0=in0, scalar=scalar, in1=in1, op0=op0, op1=op1),
                 reads=R, writes=Wr)

        def cp(eng, out, in_, R, Wr):
            S.op(eng, lambda e: e.tensor_copy(out=out, in_=in_), reads=R, writes=Wr)

        def memset(eng, ap, val, Wr):
            S.op(eng, lambda e: e.memset(ap, val), writes=Wr)

        P_ALL = (0, 9)
        o_cons = A.alloc(6144, *P_ALL)
        cb = [o_cons]

        def cons(dt, n, shape=None):
            esz = 2 if dt == BF16 else 4
            v = V(cb[0], dt, n, shape)
            cb[0] += (n * esz + 31) // 32 * 32
            assert cb[0] <= o_cons + 6144
            return v
        ones_bf = cons(BF16, 128)
        ident_bf = cons(BF16, 128)
        onesf = cons(F32, 128)
        antiI = cons(F32, 128)
        modT = cons(F32, 48)
        adab = cons(F32, 48)
        gains = cons(F32, 24)
        small = cons(F32, 24)
        cT = cons(F32, 8)
        scT = cons(BF16, 8)
        gm1 = cons(F32, 8)
        gm2 = cons(F32, 8)
        hv = cons(F32, 4)
        esink = cons(F32, 16)
        epst = cons(F32, 1)
        relT = cons(F32, 16)
        eoh = cons(F32, 128)
        sh2b = cons(BF16, 8)
        b1T = cons(F32, 32)
        bgT = small[:, 0:16]
        gqT = small[:, 16:18]
        gkvT = small[:, 18:19]
        ropec = small[:, 19:22]
        g1T = gains[:, 0:8]
        g2T = gains[:, 8:16]
        gfT = gains[:, 16:24]
        sh1 = modT[:, 0:8]
        ga1 = modT[:, 16:24]
        sh2 = modT[:, 24:32]
        ga2 = modT[:, 40:48]

        S.dma("sp", cT, ct_d, "c0", writes=["cT"])
        S.dma("sp", adab, adab_d, "c1", writes=["adab"])
        S.dma("sp", gains, gains_d, "c2", writes=["gains"])
        S.dma("sp", small, small_d, "c3", writes=["small"])
        S.dma("sp", hv, hv_d, "c4", writes=["hv"])
        S.dma("sp", relT[0:32, :], relT_d, "c5", writes=["relT"])
        S.dma("sp", eoh[0:32, :], eoh_d, "c6", writes=["eoh"])
        S.dma("sp", antiI, antiI_d, "c7", writes=["antiI"])
        S.dma("sp", esink, sinks_d.partition_broadcast(128), "c8", writes=["esink"])
        S.dma("pool", ident_bf, ident_d, "c9", writes=["ident"])
        memset("pool", ones_bf, 1.0, ["ones_bf"])
        memset("pool", onesf, 1.0, ["onesf"])
        memset("pool", epst, EPS, ["epst"])

        NSLOT = 3
        SLOT = 8192
        o_ring = A.alloc(NSLOT * SLOT, *P_ALL)
        pieces = []

        def slot_view(s, shape, parts=(0, 128), col0=0):
            n = 1
            for x in shape:
                n *= x
            return V(o_ring + s * SLOT + col0 * 2, BF16, n, shape, parts)

        class Ring:
            def __init__(self):
                self.next_load = 0
                self.next_rel = 0

            def _load(self):
                i = self.next_load
                if i >= len(pieces):
                    return
                s = i % NSLOT
                for (shape, parts, col0, src) in pieces[i]:
                    S.dma("pool", slot_view(s, shape, parts, col0), src, ("ring", s), writes=[("ring", s)],
                          group=(len(pieces[i]) > 1))
                self.next_load += 1

            def start(self):
                for _ in range(NSLOT):
                    self._load()

            def slot(self, i):
                assert i < self.next_load, (i, self.next_load)
                return i % NSLOT

            def release(self, i):
                assert i == self.next_rel
                self.next_rel += 1
                self._load()
        RING = Ring()

        def kc_src(w, c0, c1):
            return w.rearrange("(kc p) n -> p kc n", p=128)[:, :, c0:c1]

        def add_piece(subs):
            pieces.append(subs)
            return len(pieces) - 1

        PI_ADA = [None] * 12
        for i in range(4):
            PI_ADA[i] = add_piece([([8, 512], (0, 128), 0, kc_src(adaw_d, 512 * i, 512 * i + 512))])
        PI_WQ = add_piece([([8, 256], (0, 128), 0, kc_src(win_d, 0, 256))])
        for i in range(4, 12):
            PI_ADA[i] = add_piece([([8, 512], (0, 128), 0, kc_src(adaw_d, 512 * i, 512 * i + 512))])
        PI_QS = [None, None]
        PI_QS[0] = add_piece([([8, 512], (0, 128), 0, kc_src(win_d, 576, 576 + 512))])
        PI_KSVS = add_piece([([8, 256], (0, 128), 0, kc_src(win_d, 1600, 1856))])
        PI_QS[1] = add_piece([([8, 512], (0, 128), 0, kc_src(win_d, 576 + 512, 576 + 1024))])
        PI_MA = []
        for m in range(8):
            PI_MA.append(add_piece([
                ([8, 128], (0, 128), 0, kc_src(win_d, 1856 + 128 * m, 1856 + 128 * m + 128)),
                ([8, 128], (0, 128), 1024, kc_src(win_d, 2880 + 128 * m, 2880 + 128 * m + 128)),
                ([8, 128], (0, 128), 2048, kc_src(woswa_d, 128 * m, 128 * m + 128)),
                ([4, 128], (0, 128), 3072, womla_d.rearrange("(j p) n -> p j n", p=128)[:, :, 128 * m:128 * m + 128]),
            ]))
        PI_WO = [add_piece([([8, 512], (0, 128), 0, kc_src(wo_d, 512 * i, 512 * i + 512))]) for i in range(2)]
        PI_FF = []
        for hf in range(2):
            a = [add_piece([([8, 512], (0, 128), 0, kc_src(wff1_d, 2048 * hf + 512 * i, 2048 * hf + 512 * i + 512))])
                 for i in range(4)]
            b = [add_piece([([4, 1024], (0, 128), 0,
                             wff2_d.rearrange("(kc p) n -> p kc n", p=128)[:, 16 * hf + 4 * i:16 * hf + 4 * i + 4, :])])
                 for i in range(4)]
            PI_FF.append((a, b))
        RING.start()
        o_kvx = A.alloc(8 * 320 * 2, 0, 1)
        wkvx = V(o_kvx, BF16, 8 * 320, [8, 320])
        o_kvxf = A.alloc(8 * 320 * 4, 0, 0)
        wkvxf = V(o_kvxf, F32, 8 * 320, [8, 320])
        S.dma("sp", wkvxf, kc_src(win_d, 256, 576), "wkvx", writes=["wkvxf"])

        nrm_ctr = [0]

        def norm_scratch(p0, p1, ncols, nsq=2):
            return {"sq": A.alloc(nsq * 8 * ncols * 2, p0, p1), "tmp": A.alloc(4 * ncols * 4, p0, p1),
                    "rstd": A.alloc(2 * ncols * 4, p0, p1), "w": ncols, "nsq": nsq}

        def norm_sq(NS, xv, xkey, ncols, sq_eng="pool"):
            b = nrm_ctr[0] % 2
            nrm_ctr[0] += 1
            W_ = NS["w"]
            sqb = b % NS["nsq"]
            sq = V(NS["sq"] + sqb * 8 * W_ * 2, BF16, 8 * ncols, [8, ncols])
            if sq_eng == "act":
                act(sq, xv, AF.Square, [xkey], [("sq", sqb)])
            elif sq_eng == "pool_all":
                tt("pool", sq[:, 0:4, :], xv[:, 0:4, :], xv[:, 0:4, :], ALU.mult, [xkey], [("sq", sqb, 0)])
                tt("pool", sq[:, 4:8, :], xv[:, 4:8, :], xv[:, 4:8, :], ALU.mult, [xkey], [("sq", sqb, 1)])
            else:
                tt("pool", sq[:, 0:4, :], xv[:, 0:4, :], xv[:, 0:4, :], ALU.mult, [xkey], [("sq", sqb, 0)])
                act(sq[:, 4:8, :], xv[:, 4:8, :], AF.Square, [xkey], [("sq", sqb, 1)])
            return b

        def norm_ones(NS, b, ncols, subranges, psbanks):
            W_ = NS["w"]
            sqb = b % NS["nsq"]
            sq = V(NS["sq"] + sqb * 8 * W_ * 2, BF16, 8 * ncols, [8, ncols])
            for si, (c0, c1) in enumerate(subranges):
                pb = psbanks[si % len(psbanks)]
                for kc in range(8):
                    mm(PS(pb)[:, 0:c1 - c0], ones_bf, sq[:, kc, c0:c1], kc == 0, kc == 7,
                       [("sq", sqb), ("sq", sqb, 0), ("sq", sqb, 1), "ones_bf"], [("ps", pb)])

        def norm_stats_a(NS, xv, xkey, ncols, subranges, psbanks, sq_eng="pool"):
            b = norm_sq(NS, xv, xkey, ncols, sq_eng)
            norm_ones(NS, b, ncols, subranges, psbanks)
            return b

        def norm_stats_b(NS, b, ncols, subranges, psbanks):
            W_ = NS["w"]
            rstd = V(NS["rstd"] + b * W_ * 4, F32, ncols)
            for si, (c0, c1) in enumerate(subranges):
                pb = psbanks[si % len(psbanks)]
                act(rstd[:, c0:c1], PS(pb)[:, 0:c1 - c0], AF.Ln, [("ps", pb), "epst"], [("rstd", b)],
                    scale=1.0 / D, bias=epst[:, 0:1])
            act(rstd, rstd, AF.Exp, [("rstd", b)], [("rstd", b)], scale=-0.5)

        def norm_stats(NS, xv, xkey, ncols, subranges, psbanks, sq_eng="pool"):
            b = norm_stats_a(NS, xv, xkey, ncols, subranges, psbanks, sq_eng)
            norm_stats_b(NS, b, ncols, subranges, psbanks)
            return b

        def norm_apply(NS, b, xv, xkey, ncols, gm, sh, out_fn, okey, modkey, pool_kcs=()):
            W_ = NS["w"]
            rstd = V(NS["rstd"] + b * W_ * 4, F32, ncols)
            cnt = {"dve": 0, "pool": 0}
            for kc in range(8):
                eng = "pool" if kc in pool_kcs else "dve"
                tb = (cnt[eng] % 2) + (2 if eng == "pool" else 0)
                cnt[eng] += 1
                tmp = V(NS["tmp"] + tb * W_ * 4, F32, ncols)
                tt(eng, tmp, xv[:, kc, :], rstd, ALU.mult, [xkey, ("rstd", b)], [("ntmp", tb)])
                if sh is not None:
                    act(out_fn(kc), tmp, AF.Identity, [("ntmp", tb), modkey], [okey],
                        scale=gm[:, kc:kc + 1], bias=sh[:, kc:kc + 1])
                else:
                    act(out_fn(kc), tmp, AF.Identity, [("ntmp", tb), modkey], [okey], scale=gm[:, kc:kc + 1])

        o_bias = A.alloc(2 * 16 * 128 * 2, 0, 4)
        bias8 = V(o_bias, BF16, 2 * 16 * 128, [2, 16, 128])
        o_bt = A.alloc(2 * 16 * 128 * 4 + 384 * 4, 0, 0)
        btoe = V(o_bt, F32, 2 * 16 * 128, [2, 16, 128])
        wtab = V(o_bt + 2 * 16 * 128 * 4, F32, 384)
        mm(PS(6)[0:16, 0:128], relT[0:32, :], eoh[0:32, :], True, True, ["relT", "eoh"], [("ps", 6)])
        memset("pool", wtab[0:16, :], NEG * 8.0, ["wtab"])
        act(wtab[0:16, 127:255], PS(6)[0:16, 0:128], AF.Copy, [("ps", 6)], ["wtab"], scale=8.0)
        S.dma("sp", scr_d, wtab[0:16, :], "c11", reads=["wtab"], writes=["scr"])
        for sel, base in ((0, 255), (1, 127)):
            for hq in range(4):
                src = bass.AP(scr_d.tensor, base - 127 + 4 * hq * 384, [[1, 128], [384, 4], [1, 128]])
                S.dma("sp", btoe[:, sel, 4 * hq:4 * hq + 4, :], src, "c12", reads=["scr"], writes=["btoe"], group=True)

        act(scT, cT, AF.Silu, ["cT"], ["scT"])

        def ada_piece(i):
            s = RING.slot(PI_ADA[i])
            w = slot_view(s, [8, 512])
            for jj in range(4):
                j = 4 * i + jj
                for kc in range(8):
                    mm(PS(7)[:, j:j + 1], w[:, kc, 128 * jj:128 * jj + 128], scT[:, kc:kc + 1], kc == 0, kc == 7,
                       [("ring", s), "scT"], [("ps", 7)])
            tt("dve", modT[:, 4 * i:4 * i + 4], PS(7)[:, 4 * i:4 * i + 4], adab[:, 4 * i:4 * i + 4], ALU.add,
               [("ps", 7), "adab"], ["mod" if i < 4 else "mod2"])
            RING.release(PI_ADA[i])

        o_ropek = A.alloc(2 * S_LEN * 4, 0, 1)
        cosK = V(o_ropek, F32, S_LEN)
        sinK = V(o_ropek + S_LEN * 4, F32, S_LEN)
        RW = 1536
        o_rt = A.alloc(6 * RW * 4, 0, 0)
        RP = slice(64, 96)
        posi = V(o_rt, I32, RW)
        posf = V(o_rt + RW * 4, F32, RW)
        angt = V(o_rt + 2 * RW * 4, F32, RW)
        kf = V(o_rt + 3 * RW * 4, F32, RW)
        rtab = [V(o_rt + (4 + i) * RW * 4, F32, RW) for i in range(2)]
        for tg in range(4):
            pr = slice(32 * tg, 32 * tg + 32)
            S.dma("sp", posi[pr, 0:1024], pos_d[:, 1024 * tg:1024 * tg + 1024].partition_broadcast(32), "c10",
                  writes=["posi"], group=True)
            S.dma("sp", posi[pr, 1024:RW], pos_d[:, S_LEN + 512 * tg:S_LEN + 512 * tg + 512].partition_broadcast(32),
                  "c10", writes=["posi"], group=True)
        cp("dve", posf, posi, ["posi"], ["posf"])
        for col in range(2):
            ts("dve", angt, posf, ropec[:, 0:1], ropec[:, 1 + col:2 + col], ALU.mult, ALU.add,
               ["posf", "small"], ["angt"])
            ts("dve", kf, angt, 1.0 / TWO_PI, None, ALU.mult, None, ["angt"], ["kf"])
            cp("dve", posi, kf, ["kf"], ["kint", "posi"])
            cp("dve", kf, posi, ["kint"], ["kf"])
            stt(angt, kf, -CW1, angt, ALU.mult, ALU.add, ["kf", "angt"], ["angt"])
            stt(angt, kf, -CW2, angt, ALU.mult, ALU.add, ["kf", "angt"], ["angt"])
            ts("dve", kf, angt, math.pi, -TWO_PI, ALU.is_gt, ALU.mult, ["angt"], ["kf"])
            tt("dve", angt, angt, kf, ALU.add, ["angt", "kf"], ["angt"])
            ts("dve", kf, angt, -math.pi, TWO_PI, ALU.is_lt, ALU.mult, ["angt"], ["kf"])
            tt("dve", angt, angt, kf, ALU.add, ["angt", "kf"], ["angt"])
            ts("dve", angt, angt, math.pi, -math.pi, ALU.min, ALU.max, ["angt"], ["angt"])
            act(rtab[col], angt, AF.Sin, ["angt"], [("rtab", col)])
            tK = (cosK, sinK)[col]
            for tg in range(4):
                pr = slice(32 * tg, 32 * tg + 32)
                S.dma("sp", tK[RP, 1024 * tg:1024 * tg + 1024], rtab[col][pr, 0:1024], ("ropeK", col),
                      reads=[("rtab", col)], writes=["ropeK%d" % col], group=True)
                S.dma("sp", ropeq_d[col, :, 512 * tg:512 * tg + 512], rtab[col][pr, 1024:RW], ("ropeQs", col),
                      reads=[("rtab", col)], writes=["ropeQscr%d" % col], group=True)

        for q in range(8):
            flat = btoe.rearrange("p a b c -> p (a b c)")
            mm(PS(6)[:, :], antiI, flat[:, 512 * q:512 * q + 512], True, True, ["antiI", "btoe"], [("ps", 6)])
            cp("dve", bias8.rearrange("p a b c -> p (a b c)")[:, 512 * q:512 * q + 512], PS(6)[:, :],
               [("ps", 6)], ["bias8"])
        act(esink, esink, AF.Exp, ["esink"], ["esink"])
        o_xst = A.alloc(3 * 8 * 512 * 2, 0, 1)
        xbs = [V(o_xst + i * 8 * 512 * 2, BF16, 8 * 512, [8, 512]) for i in range(3)]

        def kv_load(G):
            S.dma("pool", xbs[G % 3], xs_d.rearrange("(kc p) n -> p kc n", p=128)[:, :, 512 * G:512 * G + 512],
                  ("xst", G % 3), writes=[("xst", G % 3)])
        for i in range(4):
            ada_piece(i)
        for G in range(3):
            kv_load(G)
        stt(gm1, modT[:, 8:16], 1.0, g1T, ALU.add, ALU.mult, ["mod", "gains"], ["mod"])
        bvk = cons(F32, 3)
        for (col, c0, M) in ((0, 0, 128), (1, 128, 96), (2, 224, 96)):
            for kc in range(8):
                mm(PS(6)[0:M, col:col + 1], wkvxf[:, kc, c0:c0 + M], sh1[:, kc:kc + 1], kc == 0, kc == 7,
                   ["wkvxf", "mod"], [("ps", 6)])
            cp("dve", bvk[0:M, col:col + 1], PS(6)[0:M, col:col + 1], [("ps", 6)], ["bvk"])
        for kc in range(8):
            ts("dve", wkvx[:, kc, :], wkvxf[:, kc, :], gm1[:, kc:kc + 1], None, ALU.mult, None,
               ["wkvxf", "mod"], ["wkvx"])

        S.fence()
        o_kvn = A.alloc(S_LEN * 2, 1, 3)
        o_K = A.alloc(2 * S_LEN * 2, 1, 3)
        o_kt = A.alloc(6 * 512 * 4 + 1024, 1, 1)
        o_sq1 = A.alloc(8 * 512 * 2, 1, 1)
        o_rs1 = A.alloc(2 * 512 * 4, 1, 1)
        kvn = V(o_kvn, BF16, S_LEN)
        Kb = [V(o_K + b * S_LEN * 2, BF16, S_LEN) for b in range(2)]
        sq1 = V(o_sq1, BF16, 8 * 512, [8, 512])
        rstd1 = [V(o_rs1 + i * 2048, F32, 512) for i in range(2)]

        def kv_sq(G):
            xb = xbs[G % 3]
            act(sq1[:, 0:4, :], xb[:, 0:4, :], AF.Square, [("xst", G % 3)], [("sq1", 0)])
            tt("dve", sq1[:, 4:8, :], xb[:, 4:8, :], xb[:, 4:8, :], ALU.mult, [("xst", G % 3)], [("sq1", 1)])

        def kv_ones(G):
            pb = (G % 2) * 5
            for kc in range(8):
                mm(PS(pb)[:, :], ones_bf, sq1[:, kc, :], kc == 0, kc == 7, [("sq1", 0), ("sq1", 1), "ones_bf"],
                   [("ps", pb)])
            r = rstd1[G % 2]
            act(r, PS(pb)[:, :], AF.Ln, [("ps", pb), "epst"], [("rstd1", G % 2)], scale=1.0 / D, bias=epst[:, 0:1])
            act(r, r, AF.Exp, [("rstd1", G % 2)], [("rstd1", G % 2)], scale=-0.5)

        def kv_proj(G):
            xb = xbs[G % 3]
            for (pb, c0, M) in ((1, 0, 128), (2, 128, 96), (3, 224, 96)):
                for kc in range(8):
                    mm(PS(pb)[0:M, :], wkvx[:, kc, c0:c0 + M], xb[:, kc, :], kc == 0, kc == 7,
                       ["wkvx", ("xst", G % 3)], [("ps", pb)])
            if G + 3 < 8:
                kv_load(G + 3)

        def kv_back(G):
            cs = slice(512 * G, 512 * G + 512)
            r = rstd1[G % 2]
            rkey = ("rstd1", G % 2)
            sqk = V(o_kt, BF16, 512)
            rk = V(o_kt + 1024, F32, 512)
            tkv = V(o_kt + 1024 + 2048, F32, 512)
            tb = G % 2
            t1 = V(o_kt + 1024 + 4096 + tb * 4096, F32, 512)
            t2 = V(o_kt + 1024 + 6144 + tb * 4096, F32, 512)
            tt("dve", tkv, PS(1)[:, :], r, ALU.mult, [("ps", 1), rkey], ["tkv"])
            tt("dve", t1[RP, :], PS(2)[RP, :], r[RP, :], ALU.mult, [("ps", 2), rkey], [("kt1", tb)])
            tt("dve", t2[RP, :], PS(3)[RP, :], r[RP, :], ALU.mult, [("ps", 3), rkey], [("kt2", tb)])
            if G > 0:
                kv_adds(G - 1)
            act(sqk, tkv, AF.Square, ["tkv", "bvk"], ["sqk"], bias=bvk[:, 0:1])
            mm(PS(4)[:, :], ones_bf, sqk, True, True, ["sqk", "ones_bf"], [("ps", 4)])
            act(rk, PS(4)[:, :], AF.Ln, [("ps", 4), "epst"], ["rk"], scale=1.0 / 128, bias=epst[:, 0:1])
            act(rk, rk, AF.Exp, ["rk"], ["rk"], scale=-0.5)
            stt(t1[RP, :], t1[RP, :], bvk[RP, 1:2], cosK[RP, cs], ALU.add, ALU.mult,
                [("kt1", tb), "bvk", "ropeK0"], [("kt1", tb)])
            stt(t2[RP, :], t2[RP, :], bvk[RP, 2:3], sinK[RP, cs], ALU.add, ALU.mult,
                [("kt2", tb), "bvk", "ropeK1"], [("kt2", tb)])
            stt(kvn[:, cs], tkv, bvk[:, 0:1], rk, ALU.add, ALU.mult, ["tkv", "rk", "bvk"], [("kvn", G)])

        def kv_adds(G):
            cs = slice(512 * G, 512 * G + 512)
            tb = G % 2
            t1 = V(o_kt + 1024 + 4096 + tb * 4096, F32, 512)
            t2 = V(o_kt + 1024 + 6144 + tb * 4096, F32, 512)
            tt("dve", Kb[0][RP, cs], t1[RP, :], t2[RP, :], ALU.add, [("kt1", tb), ("kt2", tb)], [("Kpe", 0, G)])
            tt("dve", Kb[1][RP, cs], t1[RP, :], t2[RP, :], ALU.add, [("kt1", tb), ("kt2", tb)], [("Kpe", 1, G)])

        kv_sq(0)
        kv_ones(0)
        for G in range(8):
            if G + 1 < 8:
                kv_sq(G + 1)
            kv_proj(G)
            if G + 1 < 8:
                kv_ones(G + 1)
            kv_back(G)
        kv_adds(7)
        o_xst2 = A.alloc(2 * 8 * GW * 4, 1, 2)
        xst2s = [V(o_xst2 + i * 8 * GW * 4, F32, 8 * GW, [8, GW]) for i in range(2)]

        def own_load(g):
            S.dma("sp", xst2s[g % 2], xo_d.rearrange("(kc p) n -> p kc n", p=128)[:, :, GW * g:GW * g + GW],
                  ("xst2", g % 2), writes=[("xst2", g % 2)])
        own_load(0)
        own_load(1)

        S.fence()
        o_ho = A.alloc(8 * 4 * GW * 2, 2, 5)
        o_qn = A.alloc(2 * T * 2, 2, 3)
        ho = V(o_ho, BF16, 8 * 4 * GW, [8, 4, GW])
        qn = V(o_qn, BF16, 2 * T, [2, T])
        o_wu = A.alloc(2 * 1536 * 2 + 1024 * 2, 2, 3)
        wuq = V(o_wu, BF16, 2 * 1536, [2, 1536])
        wukv = V(o_wu + 2 * 1536 * 2, BF16, 1024)
        o_wuf = A.alloc(1024 * 4, 2, 2)
        wukvf = V(o_wuf, F32, 1024)
        S.dma("sp", wukvf, wukv_d, "wukv", writes=["wukvf"])
        S.dma("pool", wuq, wuq_d.rearrange("(kc p) n -> p kc n", p=128), "wuq", writes=["wuq"])
        swq = RING.slot(PI_WQ)
        wq = slot_view(swq, [8, 256])
        o_qt = A.alloc(2 * 512 * 4 + 2 * 512 * 2, 2, 2)
        NS2 = norm_scratch(2, 2, GW, 1)
        sqq = V(o_qt, BF16, 2 * 512, [2, 512])
        rq = V(o_qt + 2048, F32, 512)

        own_nb = {}

        def own_sq(g):
            own_nb[g] = norm_sq(NS2, xst2s[g % 2], ("xst2", g % 2), GW)

        def own_ones(g):
            norm_ones(NS2, own_nb[g], GW, [(0, 128), (128, GW)], [0, 1])
            norm_stats_b(NS2, own_nb[g], GW, [(0, 128), (128, GW)], [0, 1])

        def own_apply(g):
            norm_apply(NS2, own_nb[g], xst2s[g % 2], ("xst2", g % 2), GW, gm1, sh1,
                       lambda kc, g=g: ho[:, kc, g, :], ("ho", g), "mod")
            if g + 2 < 4:
                own_load(g + 2)

        def own_proj(g):
            for mc in range(2):
                for kc in range(8):
                    mm(PS(2 + mc)[:, :], wq[:, kc, 128 * mc:128 * mc + 128], ho[:, kc, g, 128:GW], kc == 0, kc == 7,
                       [("ring", swq), ("ho", g)], [("ps", 2 + mc)])

        def own_back(g):
            for mc in range(2):
                act(sqq[:, mc, :], PS(2 + mc)[:, :], AF.Square, [("ps", 2 + mc)], ["sqq"])
            for mc in range(2):
                mm(PS(4)[:, :], ones_bf, sqq[:, mc, :], mc == 0, mc == 1, ["sqq", "ones_bf"], [("ps", 4)])
            act(rq, PS(4)[:, :], AF.Ln, [("ps", 4), "epst"], ["rq"], scale=1.0 / 256, bias=epst[:, 0:1])
            act(rq, rq, AF.Exp, ["rq"], ["rq"], scale=-0.5)
            for mc in range(2):
                stt(qn[:, mc, 512 * g:512 * g + 512], PS(2 + mc)[:, :], gqT[:, mc:mc + 1], rq, ALU.mult, ALU.mult,
                    [("ps", 2 + mc), "rq", "small"], [("qn", g)])

        own_sq(0)
        own_ones(0)
        own_apply(0)
        for g in range(4):
            if g + 1 < 4:
                own_sq(g + 1)
            own_proj(g)
            if g + 1 < 4:
                own_ones(g + 1)
            own_back(g)
            if g + 1 < 4:
                own_apply(g + 1)
        RING.release(PI_WQ)
        ts("dve", wukv, wukvf, gkvT[:, 0:1], None, ALU.mult, None, ["wukvf", "small"], ["wukv"])

        S.fence()
        SC_MLA = 96.0 ** -0.5
        o_ymla = A.alloc(8 * T * 2, 3, 5)
        cosQ = V(o_ymla, F32, T)
        sinQ = V(o_ymla + T * 4, F32, T)
        S.dma("sp", cosQ[RP, :], ropeq_d[0], "ropeQ0", reads=["ropeQscr0"], writes=["ropeQ0"])
        S.dma("sp", sinQ[RP, :], ropeq_d[1], "ropeQ1", reads=["ropeQscr1"], writes=["ropeQ1"])
        o_Vh = A.alloc(2 * 32 * 65 * 2, 3, 3)
        o_Q = A.alloc(2 * T * 2, 3, 3)
        o_mask = A.alloc(16 * 512 * 2, 3, 3)
        o_P = A.alloc(4 * 512 * 2, 3, 4)
        o_ep = A.alloc(4 * 512 * 4, 3, 3)
        Vh = [V(o_Vh + b * 32 * 65 * 2, BF16, 32 * 65, [32, 65]) for b in range(2)]
        Qb = [V(o_Q + b * T * 2, BF16, T) for b in range(2)]
        masks = V(o_mask, BF16, 16 * 512, [16, 512])
        Pt = [V(o_P + i * 1024, BF16, 512) for i in range(4)]
        rrows = [V(o_ep, F32, 512)] * 2
        bcs = [V(o_ep + 2048, F32, 512)] * 2
        ectr = [0]
        rt1 = V(o_ep + 2 * 2048, F32, 512)
        rt2 = V(o_ep + 3 * 2048, F32, 512)
        ymla = V(o_ymla, BF16, 8 * T, [8, T])
        for mq in range(4):
            S.dma("pool", masks[:, 4 * mq:4 * mq + 4, :], masks_d.rearrange("m p n -> p m n")[:, 4 * mq:4 * mq + 4, :],
                  "c13", writes=["masks"], group=True)
        for b in range(2):
            memset("pool", Vh[b][:, :, 64:65], 1.0, [("V", b)])
        pctr = [0]
        sctr = [0]
        actr = [0]
        bctr = [0]

        def build_steps(h):
            b = h % 2
            steps = []
            for G in range(8):
                def kstep(G=G):
                    pb = 5 + bctr[0] % 2
                    bctr[0] += 1
                    cs = slice(512 * G, 512 * G + 512)
                    mm(PS(pb)[0:64, :], wukv[:, 64 * h:64 * h + 64], kvn[:, cs], True, True,
                       ["wukv", ("kvn", G)], [("ps", pb)])
                    cp("dve", Kb[b][0:64, cs], PS(pb)[0:64, :], [("ps", pb)], [("K", b)])
                steps.append(kstep)
            for q4 in range(8):
                def vstep(q4=q4):
                    pb = 5 + bctr[0] % 2
                    bctr[0] += 1
                    for j in range(4):
                        blk = 4 * q4 + j
                        mm(PS(pb)[:, 64 * j:64 * j + 64], kvn[:, 128 * blk:128 * blk + 128],
                           wukv[:, 512 + 64 * h:512 + 64 * h + 64], True, True,
                           ["wukv", ("kvn", blk // 4)], [("ps", pb)], sig=(j == 3))
                    cp("dve", Vh[b][:, 4 * q4:4 * q4 + 4, 0:64],
                       PS(pb)[:, 0:256].rearrange("p (a b) -> p a b", a=4), [("ps", pb)], [("V", b)])
                steps.append(vstep)
            for g in range(4):
                def qstep(g=g):
                    cs = slice(512 * g, 512 * g + 512)
                    for (pb, c0) in ((5, 96 * h), (6, 768 + 96 * h)):
                        for kc in range(2):
                            mm(PS(pb)[0:96, :], wuq[:, kc, c0:c0 + 96], qn[:, kc, cs], kc == 0, kc == 1,
                               ["wuq", ("qn", g)], [("ps", pb)])
                    cp("dve", Qb[b][0:64, cs], PS(5)[0:64, :], [("ps", 5)], [("Q", b)])
                    tt("dve", rt1[RP, :], PS(5)[RP, :], cosQ[RP, cs], ALU.mult, [("ps", 5), "ropeQ0"], ["rt1"])
                    tt("dve", rt2[RP, :], PS(6)[RP, :], sinQ[RP, cs], ALU.mult, [("ps", 6), "ropeQ1"], ["rt2"])
                    tt("dve", Qb[b][RP, cs], rt1[RP, :], rt2[RP, :], ALU.add, ["rt1", "rt2"], [("Q", b)])
                steps.append(qstep)
            return steps

        items = []
        for h in range(8):
            for c in range(4):
                for kb in range(8 * c + 8):
                    items.append((h, c, kb))
        LOOK = 2
        st_sb = {}
        st_pi = {}
        st_ab = {}
        deferred = []
        for stp in build_steps(0):
            stp()
        pend_steps = []

        def mla_front(i):
            h, c, kb = items[i]
            b = h % 2
            if c == 0 and kb == 0:
                if h + 1 < 8:
                    pend_steps.extend(build_steps(h + 1))
            if c == 1 and kb == 0:
                ada_piece(4 + h)
            if kb == 0:
                st_ab[(h, c)] = actr[0] % 2
                actr[0] += 1
            sb = 2 + sctr[0] % 3
            sctr[0] += 1
            st_sb[i] = sb
            qs = slice(512 * c, 512 * c + 512)
            masked = kb >= 8 * c
            mm(PS(sb)[:, :], Kb[b][0:96, 128 * kb:128 * kb + 128], Qb[b][0:96, qs], True, not masked,
               [("K", b), ("Kpe", b, kb // 4), ("Q", b)], [("ps", sb)])
            if masked:
                mm(PS(sb)[:, :], ident_bf, masks[:, 8 * (c % 2) + kb - 8 * c, :], False, True,
                   ["ident", "masks"], [("ps", sb)])
            pi = pctr[0] % 4
            pctr[0] += 1
            st_pi[i] = pi
            act(Pt[pi], PS(sb)[:, :], AF.Exp, [("ps", sb)], [("P", pi)], scale=SC_MLA)

        def mla_back(i, now):
            h, c, kb = items[i]
            b = h % 2
            nkb = 8 * c + 8
            ab = st_ab[(h, c)]
            pi = st_pi[i]
            qs = slice(512 * c, 512 * c + 512)
            mm(PS(ab)[0:65, :], Vh[b][:, kb, :], Pt[pi], kb == 0, kb == nkb - 1,
               [("V", b), ("P", pi)], [("ps", ab)])
            if kb == nkb - 1:
                ri = 0
                rr = rrows[ri]
                S.op("dve", lambda e, ab=ab, rr=rr: e.reciprocal(out=rr[64:65, :], in_=PS(ab)[64:65, :]),
                     reads=[("ps", ab)], writes=[("rrow", ri)])
                S.dma("sp", bc_d[ri], rr[64:65, :], ("bcw", ri), reads=[("rrow", ri)], writes=[("bcd", ri)])
                S.dma("sp", bcs[ri][0:64, :], bc_d[ri].partition_broadcast(64), ("bcr", ri),
                      reads=[("bcd", ri)], writes=[("bcs", ri)])

                def stage_b(ab=ab, h=h, qs=qs, ri=ri):
                    tt("dve", ymla[0:64, h, qs], PS(ab)[0:64, :], bcs[ri][0:64, :], ALU.mult,
                       [("ps", ab), ("bcs", ri)], ["ymla"])
                deferred.append((now + 7, stage_b))

        n_items = len(items)
        for i in range(n_items + LOOK + 9):
            if i < n_items:
                mla_front(i)
            if LOOK <= i < n_items + LOOK:
                mla_back(i - LOOK, i)
            for (due, fn) in list(deferred):
                if due <= i:
                    deferred.remove((due, fn))
                    fn()
            if i % 3 == 2 and pend_steps:
                pend_steps.pop(0)()
        assert not deferred and not pend_steps
        for j in range(4):
            S.dma("sp", ymla[64:128, 2 * j, :], ymla[0:64, 2 * j + 1, :], ("ypk", j), reads=["ymla"],
                  writes=[("ymla_pk", j), "ropeQ0", "ropeQ1"])
        stt(gm2, modT[:, 32:40], 1.0, g2T, ALU.add, ALU.mult, ["mod2", "gains"], ["mod2"])
        cp("dve", sh2b, modT[:, 24:32], ["mod2"], ["sh2b"])

        S.fence()
        o_qs = A.alloc(4 * T * 2, 4, 4)
        o_ks = A.alloc(4 * GW * 2, 4, 4)
        o_vs = A.alloc(20 * 2 * 128 * 2, 4, 4)
        o_yswa = A.alloc(8 * T * 2, 4, 5)
        o_rs = A.alloc(4 * 512 * 4, 4, 4)
        qsT = V(o_qs, BF16, 4 * T, [4, T])
        ksT = V(o_ks, BF16, 4 * GW, [4, GW])
        vsP = V(o_vs, BF16, 20 * 2 * 128, [20, 2, 128])
        yswa = V(o_yswa, BF16, 8 * T, [8, T])
        memset("pool", vsP, 0.0, ["vsP"])

        def qs_project(i2):
            s = RING.slot(PI_QS[i2])
            w = slot_view(s, [8, 512])
            for mcl in range(4):
                for g in range(4):
                    pb = (mcl * 4 + g) % 4
                    for kc in range(8):
                        mm(PS(pb)[:, :], w[:, kc, 128 * mcl:128 * mcl + 128], ho[:, kc, g, 128:GW], kc == 0, kc == 7,
                           [("ring", s), ("ho", g)], [("ps", pb)])
                    if g % 2 == 0:
                        act(qsT[:, mcl, 512 * g:512 * g + 512], PS(pb)[:, :], AF.Copy, [("ps", pb)], ["qsT"])
                    else:
                        cp("dve", qsT[:, mcl, 512 * g:512 * g + 512], PS(pb)[:, :], [("ps", pb)], ["qsT"])
            RING.release(PI_QS[i2])

        qs_project(0)
        s = RING.slot(PI_KSVS)
        wkv2 = slot_view(s, [8, 256])
        for g in range(4):
            for (c0, c1, pb) in ((0, 512, 4), (512, GW, 5)):
                for kc in range(8):
                    mm(PS(pb)[:, 0:c1 - c0], wkv2[:, kc, 0:128], ho[:, kc, g, c0:c1], kc == 0, kc == 7,
                       [("ring", s), ("ho", g)], [("ps", pb)])
                cp("dve", ksT[:, g, c0:c1], PS(pb)[:, 0:c1 - c0], [("ps", pb)], ["ksT"])
            for j in range(5):
                blk = 5 * g + j
                pb = 6 + (blk % 2)
                for kc in range(8):
                    mm(PS(pb)[:, 0:128], ho[:, kc, g, 128 * j:128 * j + 128], wkv2[:, kc, 128:256], kc == 0, kc == 7,
                       [("ring", s), ("ho", g)], [("ps", pb)])
                for gg in range(2):
                    if j == 0:
                        ts("dve", vsP[:, blk, gg, 64 * gg:64 * gg + 64], PS(pb)[:, 64 * gg:64 * gg + 64],
                           hv[:, g:g + 1], None, ALU.mult, None, [("ps", pb), "hv"], ["vsP"])
                    else:
                        cp("dve", vsP[:, blk, gg, 64 * gg:64 * gg + 64], PS(pb)[:, 64 * gg:64 * gg + 64],
                           [("ps", pb)], ["vsP"])
        RING.release(PI_KSVS)
        o_hones = A.alloc(4 * 128 * 2, 4, 4)
        hones = V(o_hones, BF16, 4 * 128, [4, 128])
        for g in range(4):
            ts("dve", hones[:, g, :], ones_bf, hv[:, g:g + 1], None, ALU.mult, None, ["ones_bf", "hv"], ["hones"])
        sitems = [(n, half, gg, sel) for half in range(2) for n in range(16) for gg in range(2) for sel in range(2)]
        s_sb = {}
        s_pi = {}
        sdef = []

        def swa_front(i):
            n, half, gg, sel = sitems[i]
            g, j = n // 4, n % 4
            qcols = slice(512 * g + 128 * j, 512 * g + 128 * j + 128)
            rp = slice(64 * gg, 64 * gg + 64)
            if n == 0 and gg == 0 and sel == 0 and half == 1:
                qs_project(1)
            rhs_q = qsT[rp, 0:4, qcols]
            kcols = slice(128 * (j + sel), 128 * (j + sel) + 128)
            sb = sctr[0] % 4
            sctr[0] += 1
            mm(PS(sb)[:, :], ksT[rp, g, kcols], rhs_q, True, False, ["ksT", "qsT"], [("ps", sb)])
            h0 = 8 * gg + 4 * half
            mm(PS(sb)[:, :], ident_bf, bias8[:, sel, h0:h0 + 4, :], False, True, ["ident", "bias8"], [("ps", sb)])
            pi = pctr[0] % 4
            pctr[0] += 1
            s_pi[i] = pi
            act(Pt[pi], PS(sb)[:, :], AF.Exp, [("ps", sb)], [("P", pi)], scale=0.125)

        def swa_back(i, now):
            n, half, gg, sel = sitems[i]
            g, j = n // 4, n % 4
            qcols = slice(512 * g + 128 * j, 512 * g + 128 * j + 128)
            ab = 4 + n % 2
            sumb = 6 + gg
            blk = 5 * g + j + sel
            pi = s_pi[i]
            h0 = 8 * gg + 4 * half
            mm(PS(ab)[:, :], vsP[:, blk, gg, :], Pt[pi], (gg == 0 and sel == 0), (gg == 1 and sel == 1),
               ["vsP", ("P", pi)], [("ps", ab)])
            lhs1 = hones[:, g, :] if (j == 0 and sel == 0) else ones_bf
            mm(PS(sumb)[:, :], lhs1, Pt[pi], sel == 0, sel == 1, ["hones", "ones_bf", ("P", pi)], [("ps", sumb)])
            if sel == 1:
                ri = (2 * (n % 2) + gg) % 4
                rs = V(o_rs + ri * 2048, F32, 512)

                def stage_n(ri=ri, rs=rs, sumb=sumb, h0=h0):
                    tt("dve", rs.rearrange("p (a b) -> p a b", a=4), PS(sumb)[:, :].rearrange("p (a b) -> p a b", a=4),
                       esink[:, h0:h0 + 4].unsqueeze(2).broadcast_to([128, 4, 128]), ALU.add,
                       [("ps", sumb), "esink"], [("rs", ri)])
                    act(rs, rs, AF.Ln, [("rs", ri)], [("rs", ri)])
                    act(rs, rs, AF.Exp, [("rs", ri)], [("rs", ri)], scale=-1.0)
                sdef.append((now + 2, stage_n))
                if gg == 1:
                    def stage_y(ab=ab, half=half, qcols=qcols, n=n):
                        for g2 in range(2):
                            rp = slice(64 * g2, 64 * g2 + 64)
                            ri2 = (2 * (n % 2) + g2) % 4
                            rs2 = V(o_rs + ri2 * 2048, F32, 512)
                            tt("dve", yswa[rp, 4 * half:4 * half + 4, qcols],
                               PS(ab)[rp, :].rearrange("p (a b) -> p a b", a=4),
                               rs2[rp, :].rearrange("p (a b) -> p a b", a=4), ALU.mult,
                               [("ps", ab), ("rs", ri2)], ["yswa"])
                    sdef.append((now + 4, stage_y))

        ns = len(sitems)
        for i in range(ns + LOOK + 6):
            if i < ns:
                swa_front(i)
            if LOOK <= i < ns + LOOK:
                swa_back(i - LOOK, i)
            for (due, fn) in list(sdef):
                if due <= i:
                    sdef.remove((due, fn))
                    fn()
        assert not sdef

        S.fence()
        o_mg = A.alloc(8 * T * 2, 5, 6)
        o_gt = A.alloc(2 * 4 * 512 * 4, 5, 5)
        merged = V(o_mg, BF16, 8 * T, [8, T])
        it = 0
        for m in range(8):
            s = RING.slot(PI_MA[m])
            wga = slot_view(s, [8, 128], col0=0)
            wgb = slot_view(s, [8, 128], col0=1024)
            wsw = slot_view(s, [8, 128], col0=2048)
            wml = slot_view(s, [4, 128], col0=3072)
            for g in range(4):
                cs = slice(512 * g, 512 * g + 512)
                pbase = 4 * (it % 2)
                tb = it % 2
                it += 1
                gA = V(o_gt + tb * 8192, F32, 512)
                gB = V(o_gt + tb * 8192 + 2048, F32, 512)
                tA = V(o_gt + tb * 8192 + 4096, F32, 512)
                tB = V(o_gt + tb * 8192 + 6144, F32, 512)
                for kc in range(8):
                    mm(PS(pbase)[:, :], wga[:, kc, :], ho[:, kc, g, 128:GW], kc == 0, kc == 7,
                       [("ring", s), ("ho", g)], [("ps", pbase)])
                for kc in range(8):
                    mm(PS(pbase + 1)[:, :], wgb[:, kc, :], ho[:, kc, g, 128:GW], kc == 0, kc == 7,
                       [("ring", s), ("ho", g)], [("ps", pbase + 1)])
                for j in range(4):
                    mm(PS(pbase + 2)[:, :], wml[:, j, :], ymla[:, 2 * j, cs], j == 0, j == 3,
                       [("ring", s), "ymla", ("ymla_pk", j)], [("ps", pbase + 2)])
                for kc in range(8):
                    mm(PS(pbase + 3)[:, :], wsw[:, kc, :], yswa[:, kc, cs], kc == 0, kc == 7,
                       [("ring", s), "yswa"], [("ps", pbase + 3)])
                act(gA, PS(pbase)[:, :], AF.Sigmoid, [("ps", pbase), "small"], [("gA", tb)], bias=bgT[:, m:m + 1])
                act(gB, PS(pbase + 1)[:, :], AF.Sigmoid, [("ps", pbase + 1), "small"], [("gB", tb)],
                    bias=bgT[:, 8 + m:9 + m])
                tt("dve", tA, PS(pbase + 2)[:, :], gA, ALU.mult, [("ps", pbase + 2), ("gA", tb)], [("tA", tb)])
                tt("dve", tB, PS(pbase + 3)[:, :], gB, ALU.mult, [("ps", pbase + 3), ("gB", tb)], [("tB", tb)])
                tt("pool", merged[:, m, cs], tA, tB, ALU.add, [("tA", tb), ("tB", tb)], ["merged"])
            RING.release(PI_MA[m])

        S.fence()
        o_x1 = A.alloc(8 * T * 4, 6, 9)
        o_xre = A.alloc(3 * T * 4, 6, 6)
        x1 = V(o_x1, F32, 8 * T, [8, T])
        it = 0

        def xre_load(mc):
            xb = mc % 3
            xre4 = V(o_xre + xb * T * 4, F32, T, [4, 512])
            S.dma("sp", xre4, xo_d[128 * mc:128 * mc + 128, :].rearrange("p (g w) -> p g w", g=4)[:, :, 128:GW],
                  ("xre", xb), writes=[("xre", xb)])
        for mc0 in range(3):
            xre_load(mc0)
        for i2 in range(2):
            s = RING.slot(PI_WO[i2])
            w = slot_view(s, [8, 512])
            for mcl in range(4):
                mc = 4 * i2 + mcl
                for g in range(4):
                    cs = slice(512 * g, 512 * g + 512)
                    pb = it % 4
                    xb = it % 2
                    it += 1
                    xb = mc % 3
                    xre4 = V(o_xre + xb * T * 4, F32, T, [4, 512])
                    if g == 0 and mc >= 1 and mc + 2 < 8:
                        xre_load(mc + 2)
                    xre = xre4[:, g, :]
                    for kc in range(8):
                        mm(PS(pb)[:, :], w[:, kc, 128 * mcl:128 * mcl + 128], merged[:, kc, cs], kc == 0, kc == 7,
                           [("ring", s), "merged"], [("ps", pb)])
                    stt(x1[:, mc, cs], PS(pb)[:, :], ga1[:, mc:mc + 1], xre, ALU.mult, ALU.add,
                        [("ps", pb), ("xre", xb), "mod2"], [("x1", g)])
            RING.release(PI_WO[i2])

        S.fence()
        o_h2 = A.alloc(8 * T * 2, 7, 8)
        h2 = V(o_h2, BF16, 8 * T, [8, T])
        NS7 = norm_scratch(7, 7, 512)
        nbs = {}
        nbs[0] = norm_stats(NS7, x1[:, :, 0:512], ("x1", 0), 512, [(0, 512)], [0])
        for g in range(4):
            cs = slice(512 * g, 512 * g + 512)
            if g + 1 < 4:
                cs1 = slice(512 * (g + 1), 512 * (g + 1) + 512)
                nbs[g + 1] = norm_stats(NS7, x1[:, :, cs1], ("x1", g + 1), 512, [(0, 512)], [(g + 1) % 2])
            rstd7 = V(NS7["rstd"] + nbs[g] * NS7["w"] * 4, F32, 512)
            for kc in range(8):
                stt(h2[:, kc, cs], x1[:, kc, cs], gm2[:, kc:kc + 1], rstd7, ALU.mult, ALU.mult,
                    [("x1", g), ("rstd", nbs[g]), "mod2"], [("h2", g)])

        S.fence()
        o_u = A.alloc(16 * T * 2, 8, 8)
        o_rl = A.alloc(2 * 512 * 4, 8, 8)
        u = V(o_u, BF16, 16 * T, [16, T])
        it = 0
        for hf in range(2):
            pa, pbb = PI_FF[hf]
            for i4 in range(4):
                s = RING.slot(pa[i4])
                w = slot_view(s, [8, 512])
                fg0 = 16 * hf + 4 * i4
                for fl in range(4):
                    for kc in range(8):
                        mm(PS(7)[:, fg0 + fl:fg0 + fl + 1], w[:, kc, 128 * fl:128 * fl + 128], sh2b[:, kc:kc + 1],
                           kc == 0, kc == 7, [("ring", s), "sh2b"], [("ps", 7)])
                cp("dve", b1T[:, fg0:fg0 + 4], PS(7)[:, fg0:fg0 + 4], [("ps", 7)], [("b1T", fg0)])
                for fl in range(4):
                    fc = 4 * i4 + fl
                    for g in range(4):
                        cs = slice(512 * g, 512 * g + 512)
                        pb = it % 4
                        rb = it % 2
                        it += 1
                        rl = V(o_rl + rb * 2048, F32, 512)
                        for kc in range(8):
                            mm(PS(pb)[:, :], w[:, kc, 128 * fl:128 * fl + 128], h2[:, kc, cs], kc == 0, kc == 7,
                               [("ring", s), ("h2", g)], [("ps", pb)])
                        act(rl, PS(pb)[:, :], AF.Relu, [("ps", pb), ("b1T", fg0)], [("rl", rb)],
                            bias=b1T[:, fg0 + fl:fg0 + fl + 1])
                        tt("dve", u[:, fc, cs], rl, rl, ALU.mult, [("rl", rb)], [("u", g)])
                RING.release(pa[i4])
            for i4 in range(4):
                s = RING.slot(pbb[i4])
                w2 = slot_view(s, [4, 1024])
                for m in range(8):
                    for g in range(4):
                        cs = slice(512 * g, 512 * g + 512)
                        pb = 4 + it % 3
                        it += 1
                        for kl in range(4):
                            mm(PS(pb)[:, :], w2[:, kl, 128 * m:128 * m + 128], u[:, 4 * i4 + kl, cs], kl == 0, kl == 3,
                               [("ring", s), ("u", g)], [("ps", pb)])
                        stt(x1[:, m, cs], PS(pb)[:, :], ga2[:, m:m + 1], x1[:, m, cs], ALU.mult, ALU.add,
                            [("ps", pb), ("x1", g), "mod2"], [("x1", g)])
                RING.release(pbb[i4])

        S.fence()
        o_ost = A.alloc(2 * 8 * 512 * 4, 9, 9)
        NS9 = norm_scratch(9, 9, 512)
        nbs = {}
        nbs[0] = norm_stats(NS9, x1[:, :, 0:512], ("x1", 0), 512, [(0, 512)], [0])
        for g in range(4):
            cs = slice(512 * g, 512 * g + 512)
            ob = g % 2
            ost = V(o_ost + ob * 8 * 512 * 4, F32, 8 * 512, [8, 512])
            if g + 1 < 4:
                cs1 = slice(512 * (g + 1), 512 * (g + 1) + 512)
                nbs[g + 1] = norm_stats(NS9, x1[:, :, cs1], ("x1", g + 1), 512, [(0, 512)], [(g + 1) % 2])
            rstd9 = V(NS9["rstd"] + nbs[g] * NS9["w"] * 4, F32, 512)
            for kc in range(8):
                stt(ost[:, kc, :], x1[:, kc, cs], gfT[:, kc:kc + 1], rstd9, ALU.mult, ALU.mult,
                    [("x1", g), ("rstd", nbs[g]), "gains"], [("ost", ob)])
            S.dma("sp", out_d.rearrange("(kc p) n -> p kc n", p=128)[:, :, cs], ost, ("outdma", ob),
                  reads=[("ost", ob)], writes=[("out", g)])
        S.final_wait("sp", [("out", g) for g in range(4)])
        if plan is None:
            return A.solve()
        S.emit()
    return nc


def _rel_bucket_onehot():
    n = np.arange(128)
    max_exact = 16
    nf = np.maximum(n, 1).astype(np.float32)
    large = max_exact + (np.log(nf / max_exact) / math.log(128 / max_exact) * (32 - max_exact)).astype(np.int32)
    large = np.minimum(large, 31)
    bucket = np.where(n < max_exact, n, large)
    E = np.zeros((32, 128), np.float32)
    E[bucket, n] = 1.0
    return E


def _mask_tiles(core_half):
    own = OWN_CHUNKS[core_half]
    out = np.zeros((16, 128, 512), np.float32)
    for par in range(2):
        c = par
        qidx = 512 * own[c] + np.arange(512)
        for m in range(8):
            kidx = 128 * (8 * c + m) + np.arange(128)
            ok = kidx[:, None] <= qidx[None, :]
            out[8 * par + m] = np.where(ok, 0.0, NEG)
    return out


def _prep_common(inp):
    f = np.float32
    w_in = np.asarray(inp["w_in"][0], f)
    z64 = np.zeros((D, 64), f)
    qs0 = 416
    qs_cols = []
    for i in range(8):
        qs_cols.append(w_in[:, qs0 + 64 * i:qs0 + 64 * i + 64])
        qs_cols.append(w_in[:, qs0 + 64 * (8 + i):qs0 + 64 * (8 + i) + 64])
    win2 = np.concatenate(
        [w_in[:, 0:256],
         w_in[:, 256:384], z64, w_in[:, 384:416], z64, w_in[:, 400:416], w_in[:, 384:400]]
        + qs_cols
        + [w_in[:, 1440:1568], w_in[:, 1568:1696], w_in[:, 1696:2720], w_in[:, 2720:3744]], axis=1)
    assert win2.shape == (D, 3904)
    w_uq = np.asarray(inp["w_uq"][0], f)
    wa = w_uq.reshape(256, 768)
    wb = np.zeros((256, 8, 96), f)
    wb[:, :, 64:80] = w_uq[:, :, 80:96]
    wb[:, :, 80:96] = w_uq[:, :, 64:80]
    wuq = np.concatenate([wa, wb.reshape(256, 768)], axis=1)
    w_ukv = np.asarray(inp["w_ukv"][0], f)
    wukv = np.concatenate([w_ukv[:, :, 0:64].reshape(128, 512), w_ukv[:, :, 64:128].reshape(128, 512)], axis=1)
    w_o_swa = np.asarray(inp["w_o_swa"][0], f)
    rows = []
    for i in range(8):
        rows.append(w_o_swa[64 * i:64 * i + 64])
        rows.append(w_o_swa[64 * (8 + i):64 * (8 + i) + 64])
    w_o_swa_g = np.concatenate(rows, axis=0)

    def colT(v, n):
        return np.ascontiguousarray(np.asarray(v, f).reshape(n, 128).T)
    gains = np.concatenate([colT(inp["ln_mix_g"][0], 8), colT(inp["ln_mlp_g"][0], 8), colT(inp["ln_final_g"], 8)], axis=1)
    small = np.zeros((128, 24), f)
    small[:, 0:16] = colT(inp["b_gate"][0], 16)
    small[:, 16:18] = colT(inp["mla_q_norm_g"][0], 2)
    small[:, 18:19] = colT(inp["mla_kv_norm_g"][0], 1)
    invf = (10000.0 ** (-np.arange(16, dtype=f) / 16)).astype(f)
    for tg in range(4):
        small[32 * tg:32 * tg + 16, 19] = invf
        small[32 * tg + 16:32 * tg + 32, 19] = invf
        small[32 * tg:32 * tg + 32, 20] = math.pi / 2
        small[32 * tg:32 * tg + 16, 21] = math.pi
    common = {
        "ada_w": np.ascontiguousarray(inp["ada_w"][0], f),
        "adabT": colT(inp["ada_b"][0], 48),
        "gains": np.ascontiguousarray(gains),
        "win2": np.ascontiguousarray(win2),
        "wuq": np.ascontiguousarray(wuq),
        "wukv": np.ascontiguousarray(wukv),
        "w_o_mla": np.ascontiguousarray(inp["w_o_mla"][0], f),
        "w_o_swa_g": np.ascontiguousarray(w_o_swa_g),
        "w_o": np.ascontiguousarray(inp["w_o"][0], f),
        "w_ff1": np.ascontiguousarray(inp["w_ff1"][0], f),
        "w_ff2": np.ascontiguousarray(inp["w_ff2"][0], f),
        "relT": np.ascontiguousarray(np.asarray(inp["rel_bias"], f).T),
        "eoh": _rel_bucket_onehot(),
        "sinks": np.ascontiguousarray(np.asarray(inp["swa_sinks"], f).reshape(1, 16)),
        "small": small,
        "ident": np.eye(128, dtype=f),
        "antiI": np.ascontiguousarray(np.eye(128, dtype=f)[::-1]),
    }
    return common


_NC_CACHE = {}


def kernel(**inputs):
    x = np.asarray(inputs["x"], np.float32)
    c = np.asarray(inputs["c"], np.float32)
    pos = np.asarray(inputs["positions"], np.int32)
    common = _prep_common(inputs)
    in_maps = []
    for core in range(8):
        b, half = core // 2, core % 2
        own = OWN_CHUNKS[half]
        xT = np.ascontiguousarray(x[b].T)
        xo = np.zeros((D, 4, GW), np.float32)
        hv = np.ones((128, 4), np.float32)
        opos = np.zeros((T,), np.int32)
        for lc, gc in enumerate(own):
            xo[:, lc, 128:] = xT[:, 512 * gc:512 * gc + 512]
            opos[512 * lc:512 * lc + 512] = pos[b, 512 * gc:512 * gc + 512]
            if gc == 0:
                hv[:, lc] = 0.0
            else:
                xo[:, lc, 0:128] = xT[:, 512 * gc - 128:512 * gc]
        m = dict(common)
        m["xo"] = np.ascontiguousarray(xo.reshape(D, 4 * GW))
        m["xs"] = xT
        m["cT"] = np.ascontiguousarray(c[b].reshape(8, 128).T)
        m["pos"] = np.ascontiguousarray(np.concatenate([pos[b], opos])[None, :].astype(np.int32))
        m["masks"] = _mask_tiles(half)
        m["hv"] = hv
        in_maps.append(m)
    if "nc" not in _NC_CACHE:
        _NC_CACHE["nc"] = build_program(build_program(None))
    res = run_bass_kernel_spmd(_NC_CACHE["nc"], in_maps, core_ids=list(range(8)))
    out = np.zeros((4, S_LEN, D), np.float32)
    for core in range(8):
        b, half = core // 2, core % 2
        oT = np.asarray(res.results[core]["out"], np.float32)
        for lc, gc in enumerate(OWN_CHUNKS[half]):
            out[b, 512 * gc:512 * gc + 512, :] = oT[:, 512 * lc:512 * lc + 512].T
    return out
```
